# Optimizing a Trainium2 kernel written in Bass

```python
import math
import jax, jax.numpy as jnp
from jax import lax
import numpy as np

D_MODEL = 1024
BATCH = 16
SEQ = 4096
DEPTH = 2

POOL_WIDTH = 256
POOL_GROUPS = 4
POOL_GROUP_DIM = POOL_WIDTH // POOL_GROUPS
POOL_WINDOWS = (2, 4, 8, 16)
ATTN_HEADS = 4
ATTN_QK_DIM = 64
ATTN_V_DIM = 2 * ATTN_QK_DIM
ATTN_QK_WIDTH = ATTN_HEADS * 2 * ATTN_QK_DIM
ATTN_WIDTH = ATTN_HEADS * ATTN_V_DIM
Q_BLOCK = 128
LRU_WIDTH = 256
LRU_BLOCKS = 4
LRU_BLOCK_DIM = LRU_WIDTH // LRU_BLOCKS
LRU_CONV_WIDTH = 4
LRU_C = 8.0
LRU_DIRECTIONS = 2
N_BRANCH = 3
D_FF = 2816
NORM_EPS = 1e-6
SUBLN_EPS = 1e-5
IN_SPLITS = (POOL_WIDTH, ATTN_QK_WIDTH, ATTN_QK_WIDTH, ATTN_WIDTH, LRU_WIDTH, LRU_WIDTH, N_BRANCH * D_MODEL)
IN_WIDTH = 256 + 512 + 512 + 512 + 256 + 256 + 3 * 1024

kernel_name = "hybrid_pool_diffattn_rglru_encoder"


def rms_norm(x, g, eps=NORM_EPS):
    xf = x.astype(jnp.float32)
    y = xf * lax.rsqrt(jnp.mean(xf * xf, axis=-1, keepdims=True) + eps)
    return (y * g.astype(jnp.float32)).astype(x.dtype)


def swiglu(h, w_in, w_out):
    gate, up = jnp.split(h @ w_in, 2, axis=-1)
    return (jax.nn.silu(gate) * up) @ w_out


def alibi_slopes(n_heads):
    start = 2.0 ** (-8.0 / n_heads)
    return np.array([start ** (i + 1) for i in range(n_heads)], dtype=np.float32)


def pool_mixer(p, pool_w, pool_scale):
    b, s, _ = p.shape
    pf = p.astype(jnp.float32)
    csum = jnp.concatenate([jnp.zeros((b, 1, POOL_WIDTH), jnp.float32), jnp.cumsum(pf, axis=1)], axis=1)
    t = jnp.arange(s)
    outs = []
    for g, w in enumerate(POOL_WINDOWS):
        lo = jnp.clip(t - w // 2, 0, s)
        hi = jnp.clip(t + w - w // 2, 0, s)
        sl = slice(g * POOL_GROUP_DIM, (g + 1) * POOL_GROUP_DIM)
        cg = csum[..., sl]
        win_sum = jnp.take(cg, hi, axis=1) - jnp.take(cg, lo, axis=1)
        count = (hi - lo).astype(jnp.float32)[None, :, None]
        outs.append(win_sum / count - pf[..., sl])
    mixed = jnp.stack(outs, axis=2)
    mixed = jnp.einsum('bsgc,gcd->bsgd', mixed, pool_w.astype(jnp.float32)).reshape(b, s, POOL_WIDTH)
    return (mixed * pool_scale.astype(jnp.float32)).astype(p.dtype)


def diff_attention(q, k, v, lam_params, subln_g, lam_init):
    b, s, _ = q.shape
    f32 = jnp.float32
    qf = q.reshape(b, s, ATTN_HEADS, 2, ATTN_QK_DIM).astype(f32) * (ATTN_QK_DIM ** -0.5)
    kf = k.reshape(b, s, ATTN_HEADS, 2, ATTN_QK_DIM).astype(f32)
    vf = v.reshape(b, s, ATTN_HEADS, ATTN_V_DIM).astype(f32)
    lp = lam_params.astype(f32)
    lam = jnp.exp(jnp.sum(lp[0] * lp[1])) - jnp.exp(jnp.sum(lp[2] * lp[3])) + lam_init
    slopes = jnp.asarray(alibi_slopes(ATTN_HEADS))
    kpos = jnp.arange(s)
    nq = s // Q_BLOCK
    q_blocks = qf.reshape(b, nq, Q_BLOCK, ATTN_HEADS, 2, ATTN_QK_DIM).transpose(1, 0, 2, 3, 4, 5)

    def block(args):
        qb, start = args
        qpos = start + jnp.arange(Q_BLOCK)
        dist = jnp.abs(qpos[:, None] - kpos[None, :]).astype(f32)
        bias = -slopes[:, None, None, None] * dist
        scores = jnp.einsum('bqhmd,bkhmd->bhmqk', qb, kf) + bias
        probs = jax.nn.softmax(scores, axis=-1)
        wts = probs[:, :, 0] - lam * probs[:, :, 1]
        return jnp.einsum('bhqk,bkhe->bqhe', wts, vf)

    o = lax.map(block, (q_blocks, jnp.arange(nq) * Q_BLOCK))
    o = o.transpose(1, 0, 2, 3, 4).reshape(b, s, ATTN_HEADS, ATTN_V_DIM)
    o = o * lax.rsqrt(jnp.mean(o * o, axis=-1, keepdims=True) + SUBLN_EPS) * subln_g.astype(f32)
    o = o * (1.0 - lam_init)
    return o.reshape(b, s, ATTN_WIDTH).astype(q.dtype)


def rg_lru(xf, w_a, b_a, w_x, b_x, lam, reverse):
    b, s, _ = xf.shape
    f32 = jnp.float32
    xb = xf.reshape(b, s, LRU_BLOCKS, LRU_BLOCK_DIM)
    r = jax.nn.sigmoid(jnp.einsum('bsgc,gcd->bsgd', xb, w_a.astype(f32)).reshape(b, s, LRU_WIDTH) + b_a.astype(f32))
    i = jax.nn.sigmoid(jnp.einsum('bsgc,gcd->bsgd', xb, w_x.astype(f32)).reshape(b, s, LRU_WIDTH) + b_x.astype(f32))
    log_a = -LRU_C * r * jax.nn.softplus(-lam.astype(f32))
    a = jnp.exp(log_a)
    u = jnp.sqrt(-jnp.expm1(2.0 * log_a)) * (i * xf)

    def combine(c1, c2):
        a1, b1 = c1
        a2, b2 = c2
        return a1 * a2, a2 * b1 + b2

    _, h = lax.associative_scan(combine, (a, u), axis=1, reverse=reverse)
    return h


def rglru_branch(lx, lg, conv_w, conv_b, w_a, b_a, w_x, b_x, lam):
    conv = lax.conv_general_dilated(
        lx, conv_w[:, None, :], window_strides=(1,),
        padding=[(LRU_CONV_WIDTH // 2, LRU_CONV_WIDTH - 1 - LRU_CONV_WIDTH // 2)],
        dimension_numbers=('NWC', 'WIO', 'NWC'), feature_group_count=LRU_WIDTH)
    xf = (conv + conv_b).astype(jnp.float32)
    h = (rg_lru(xf, w_a[0], b_a[0], w_x[0], b_x[0], lam[0], reverse=False)
         + rg_lru(xf, w_a[1], b_a[1], w_x[1], b_x[1], lam[1], reverse=True))
    return (jax.nn.gelu(lg.astype(jnp.float32)) * h).astype(lx.dtype)


def token_mixer(h, w_in, pool_w, pool_scale, attn_lambda, attn_subln, lru_conv_w, lru_conv_b,
                lru_w_a, lru_b_a, lru_w_x, lru_b_x, lru_lambda, w_branch_pool, w_branch_attn,
                w_branch_lru, merge_bias, w_out, lam_init):
    b, s, _ = h.shape
    proj = h @ w_in
    offsets = []
    acc = 0
    for w in IN_SPLITS[:-1]:
        acc += w
        offsets.append(acc)
    p, q, k, v, lx, lg, gate_logits = jnp.split(proj, offsets, axis=-1)
    y_pool = pool_mixer(p, pool_w, pool_scale) @ w_branch_pool
    y_attn = diff_attention(q, k, v, attn_lambda, attn_subln, lam_init) @ w_branch_attn
    y_lru = rglru_branch(lx, lg, lru_conv_w, lru_conv_b, lru_w_a, lru_b_a, lru_w_x, lru_b_x, lru_lambda) @ w_branch_lru
    gates = jax.nn.sigmoid(gate_logits.reshape(b, s, N_BRANCH, D_MODEL).astype(jnp.float32)
                           + merge_bias.astype(jnp.float32)).astype(h.dtype)
    merged = gates[:, :, 0] * y_pool + gates[:, :, 1] * y_attn + gates[:, :, 2] * y_lru
    return merged @ w_out


def setup_inputs(seed: int = 0) -> dict:
    key = jax.random.key(seed)
    ks = jax.random.split(key, 32)
    L, D, F = DEPTH, D_MODEL, D_FF
    f32 = jnp.float32

    def dense(k, shape, fan_in):
        return jax.random.normal(k, shape, f32) * (fan_in ** -0.5)

    def gain(k, shape):
        return 1.0 + 0.02 * jax.random.normal(k, shape, f32)

    def small(k, shape, scale=0.01):
        return scale * jax.random.normal(k, shape, f32)

    u = jax.random.uniform(ks[16], (L, LRU_DIRECTIONS, LRU_WIDTH), f32, 0.9, 0.999)
    a0 = u ** (1.0 / LRU_C)
    lru_lambda = jnp.log(a0) - jnp.log1p(-a0)

    return {
        "x": jax.random.normal(ks[0], (BATCH, SEQ, D), f32),
        "ffn1_norm": gain(ks[1], (L, D)),
        "ffn1_w_in": dense(ks[2], (L, D, 2 * F), D),
        "ffn1_w_out": dense(ks[3], (L, F, D), F),
        "mix_norm": gain(ks[4], (L, D)),
        "w_in": dense(ks[5], (L, D, IN_WIDTH), D),
        "pool_w": dense(ks[6], (L, POOL_GROUPS, POOL_GROUP_DIM, POOL_GROUP_DIM), POOL_GROUP_DIM),
        "pool_scale": 1.0 + 0.1 * jax.random.normal(ks[7], (L, POOL_WIDTH), f32),
        "attn_lambda": small(ks[8], (L, 4, ATTN_QK_DIM), 0.1),
        "attn_subln": gain(ks[9], (L, ATTN_V_DIM)),
        "lru_conv_w": dense(ks[10], (L, LRU_CONV_WIDTH, LRU_WIDTH), LRU_CONV_WIDTH),
        "lru_conv_b": small(ks[11], (L, LRU_WIDTH)),
        "lru_w_a": dense(ks[12], (L, LRU_DIRECTIONS, LRU_BLOCKS, LRU_BLOCK_DIM, LRU_BLOCK_DIM), LRU_BLOCK_DIM),
        "lru_b_a": small(ks[13], (L, LRU_DIRECTIONS, LRU_WIDTH)),
        "lru_w_x": dense(ks[14], (L, LRU_DIRECTIONS, LRU_BLOCKS, LRU_BLOCK_DIM, LRU_BLOCK_DIM), LRU_BLOCK_DIM),
        "lru_b_x": small(ks[15], (L, LRU_DIRECTIONS, LRU_WIDTH)),
        "lru_lambda": lru_lambda,
        "w_branch_pool": dense(ks[17], (L, POOL_WIDTH, D), POOL_WIDTH),
        "w_branch_attn": dense(ks[18], (L, ATTN_WIDTH, D), ATTN_WIDTH),
        "w_branch_lru": dense(ks[19], (L, LRU_WIDTH, D), LRU_WIDTH),
        "merge_bias": small(ks[20], (L, N_BRANCH, D)),
        "w_out": dense(ks[21], (L, D, D), D),
        "ffn2_norm": gain(ks[22], (L, D)),
        "ffn2_w_in": dense(ks[23], (L, D, 2 * F), D),
        "ffn2_w_out": dense(ks[24], (L, F, D), F),
        "final_norm": gain(ks[25], (D,)),
    }


def reference(x, ffn1_norm, ffn1_w_in, ffn1_w_out, mix_norm, w_in, pool_w, pool_scale,
              attn_lambda, attn_subln, lru_conv_w, lru_conv_b, lru_w_a, lru_b_a, lru_w_x,
              lru_b_x, lru_lambda, w_branch_pool, w_branch_attn, w_branch_lru, merge_bias,
              w_out, ffn2_norm, ffn2_w_in, ffn2_w_out, final_norm):
    for l in range(DEPTH):
        lam_init = 0.8 - 0.6 * math.exp(-0.3 * l)
        x = x + 0.5 * swiglu(rms_norm(x, ffn1_norm[l]), ffn1_w_in[l], ffn1_w_out[l])
        x = x + token_mixer(rms_norm(x, mix_norm[l]), w_in[l], pool_w[l], pool_scale[l],
                            attn_lambda[l], attn_subln[l], lru_conv_w[l], lru_conv_b[l],
                            lru_w_a[l], lru_b_a[l], lru_w_x[l], lru_b_x[l], lru_lambda[l],
                            w_branch_pool[l], w_branch_attn[l], w_branch_lru[l],
                            merge_bias[l], w_out[l], lam_init)
        x = x + 0.5 * swiglu(rms_norm(x, ffn2_norm[l]), ffn2_w_in[l], ffn2_w_out[l])
    return rms_norm(x, final_norm)
```

```python
import math
from contextlib import ExitStack

import numpy as np
import ml_dtypes

import concourse.bass as bass
import concourse.mybir as mybir
from concourse.bass_utils import run_bass_kernel_spmd

F32 = mybir.dt.float32
BF16 = mybir.dt.bfloat16
AF = mybir.ActivationFunctionType
ALU = mybir.AluOpType

NCORES = 8
D = 1024
FF = 2816
S = 4096
T = 2 * S
INW = 5376
DEPTH = 2
NORM_EPS = 1e-6
SUBLN_EPS = 1e-5
SLOPES = [2.0 ** (-2.0 * (i + 1)) for i in range(4)]
POOL_WINDOWS = (2, 4, 8, 16)

DBG = {}
PE, ACT, DVE, POOL, SP = "pe", "act", "dve", "pool", "sp"
ENGS = (PE, ACT, DVE, POOL, SP)
SEM_EPOCH = 12000
DSEM_RETIRE = 12000

LV = 72
V_FFN1, V_MIX, V_FFN2, V_PSCALE, V_CONVW, V_CONVB, V_BA, V_BX, V_LAM, V_MB = 0, 8, 16, 24, 26, 34, 36, 40, 44, 48
V_FINAL = 2 * LV
NV = 2 * LV + 8


class DSem:
    __slots__ = ("sem", "count", "eng")

    def __init__(self):
        self.sem = None
        self.count = 0


class Buf:
    __slots__ = ("name", "writer", "readers", "dsem")

    def __init__(self, name=""):
        self.name = name
        self.writer = None
        self.readers = []
        self.dsem = None


class Op:
    __slots__ = ("eng", "fn", "deps", "signal", "sig", "is_dma")

    def __init__(self, eng, fn, is_dma=False):
        self.eng = eng
        self.fn = fn
        self.deps = []
        self.signal = False
        self.sig = None
        self.is_dma = is_dma


class Prog:
    def __init__(self, nc):
        self.nc = nc
        self.ops = {e: [] for e in ENGS}
        self.stage_bufs = []
        self.free_dsems = {}
        self.all_dsems = []
        self.nsem = 0

    def buf(self, name=""):
        b = Buf(name)
        self.stage_bufs.append(b)
        return b

    def _dsem(self, b, eng):
        if b.dsem is None:
            fl = self.free_dsems.setdefault(eng, [])
            if fl:
                b.dsem = fl.pop()
            else:
                b.dsem = DSem()
                b.dsem.eng = eng
                self.all_dsems.append(b.dsem)
        assert b.dsem.eng == eng, "DMA semaphore shared across issuing queues"
        return b.dsem

    def _track(self, op, reads, writes):
        deps = op.deps
        for b in reads:
            if b.writer is not None:
                deps.append(b.writer)
        for b in writes:
            if b.writer is not None:
                deps.append(b.writer)
            deps.extend(b.readers)
        for b in writes:
            b.writer = op
            b.readers = []
        for b in reads:
            r = b.readers
            if r and (not op.is_dma) and (not r[-1].is_dma) and r[-1].eng == op.eng:
                r[-1] = op
            else:
                r.append(op)
        self.ops[op.eng].append(op)
        return op

    def add(self, eng, fn, reads=(), writes=()):
        return self._track(Op(eng, fn), reads, writes)

    def dma(self, eng, pairs, owner, reads=(), writes=()):
        ds = self._dsem(owner, eng)
        op = Op(eng, None, is_dma=True)
        ds.count += 16 * len(pairs)
        op.sig = (ds, ds.count)
        op.signal = True

        def fn(e, pairs=pairs, ds=ds):
            for (o, i) in pairs:
                e.dma_start(out=o, in_=i).then_inc(ds.sem, 16)
        op.fn = fn
        return self._track(op, reads, writes)

    def barrier(self, engines=ENGS):
        alld = []
        for b in self.stage_bufs:
            if b.writer is not None:
                alld.append(b.writer)
            alld.extend(b.readers)
        for e in engines:
            op = Op(e, lambda en: en.nop())
            op.deps = list(alld)
            self.ops[e].append(op)
        for b in self.stage_bufs:
            if b.dsem is not None:
                if b.dsem.count < DSEM_RETIRE:
                    self.free_dsems.setdefault(b.dsem.eng, []).append(b.dsem)
                b.dsem = None
        self.stage_bufs = []

    def emit(self, es):
        nc = self.nc

        def newsem():
            s = es.enter_context(nc.semaphore("s%d" % self.nsem))
            self.nsem += 1
            return s

        for e in ENGS:
            for op in self.ops[e]:
                for d in op.deps:
                    if d is op:
                        continue
                    if d.eng == PE and op.eng == PE and not d.is_dma:
                        continue
                    d.signal = True
        for ds in self.all_dsems:
            ds.sem = newsem()
        for e in ENGS:
            n = 0
            sem = None
            for op in self.ops[e]:
                if op.is_dma:
                    ds, cnt = op.sig
                    op.sig = (ds.sem, cnt)
                elif op.signal:
                    if n % SEM_EPOCH == 0:
                        sem = newsem()
                    op.sig = (sem, n % SEM_EPOCH + 1)
                    n += 1
        block = es.enter_context(nc.Block())

        def run(eng_obj, e):
            waited = {}
            for op in self.ops[e]:
                need = {}
                for d in op.deps:
                    if d is op or d.sig is None:
                        continue
                    if d.eng == PE and e == PE and not d.is_dma:
                        continue
                    s, c = d.sig
                    k = id(s)
                    if waited.get(k, 0) >= c:
                        continue
                    if k not in need or need[k][1] < c:
                        need[k] = (s, c)
                for k, (s, c) in need.items():
                    eng_obj.wait_ge(s, c)
                    waited[k] = c
                if op.is_dma:
                    op.fn(eng_obj)
                else:
                    ins = op.fn(eng_obj)
                    if op.signal:
                        ins.then_inc(op.sig[0], 1)

        @block.tensor
        def _(pe):
            run(pe, PE)

        @block.scalar
        def _(a):
            run(a, ACT)

        @block.vector
        def _(v):
            run(v, DVE)

        @block.gpsimd
        def _(g):
            run(g, POOL)

        @block.sync
        def _(s):
            run(s, SP)


class PsumPool:
    def __init__(self, C, es, n, name, dtype=F32, cols=512):
        self.tiles = [es.enter_context(C.nc.psum_tensor("%s%d%s" % (name, i, C.sfx), [128, cols], dtype)) for i in range(n)]
        self.bufs = [C.P.buf(name) for _ in range(n)]
        self.i = 0
        self.n = n

    def next(self):
        i = self.i
        self.i = (i + 1) % self.n
        return self.tiles[i], self.bufs[i]


class Ctx:
    pass


def hs(h):
    return slice(h * 512, (h + 1) * 512)


def emit_norm(C, xsrc, Bx, ncols, gcol, sq, Bsq, rstd, Brstd, hT, BhT, pp, out_dtype_note=None):
    P = C.P
    P.add(ACT, lambda e: e.activation(out=sq, in_=xsrc, func=AF.Square), reads=[Bx], writes=[Bsq])
    ps, Bp = pp.next()
    for k in range(8):
        P.add(PE, lambda e, k=k: e.matmul(ps[:, :ncols], C.ones[:, :], sq[:, k, :], start=(k == 0), stop=(k == 7)),
              reads=[Bsq, C.Bconst], writes=[Bp])
    P.add(ACT, lambda e: e.activation(out=rstd, in_=ps[:, :ncols], func=AF.Sqrt, scale=1.0 / D, bias=C.epsn[:, 0:1]),
          reads=[Bp, C.Bconst], writes=[Brstd])
    P.add(DVE, lambda e: e.reciprocal(rstd, rstd), reads=[Brstd], writes=[Brstd])
    for k in range(8):
        P.add(DVE, lambda e, k=k: e.scalar_tensor_tensor(out=hT[:, k, :], in0=xsrc[:, k, :],
                                                         scalar=C.vecs[:, gcol + k:gcol + k + 1], in1=rstd,
                                                         op0=ALU.mult, op1=ALU.mult),
              reads=[Bx, Brstd, C.Bconst], writes=[BhT])


def wview(ap2d):
    return ap2d.rearrange("(c p) n -> p c n", p=128)


def stage_weights(C):
    P = C.P
    B = P.buf("wconv")
    for l in range(DEPTH):
        for name in ("ffn1_w_in", "ffn1_w_out", "w_in", "w_branch_pool", "w_branch_attn", "w_branch_lru",
                     "w_out", "ffn2_w_in", "ffn2_w_out"):
            src = C.inp[name][l]
            dst = C.wb[(name, l)]
            R, N = dst.shape
            pairs = []
            cstep = 2816 if N > 2816 else N
            for r in range(0, R, 128):
                for c in range(0, N, cstep):
                    cc = min(cstep, N - c)
                    pairs.append((dst[r:r + 128, c:c + cc], src[r:r + 128, c:c + cc]))
            P.dma(POOL, pairs, B, writes=[B])
    P.barrier()


def stage_ffn(C, l, which):
    nc, P = C.nc, C.P
    gcol = l * LV + (V_FFN1 if which == 1 else V_FFN2)
    wi = wview(C.wb[("ffn%d_w_in" % which, l)])
    wo_d = wview(C.wb[("ffn%d_w_out" % which, l)])
    xv = wview(C.xT)
    with ExitStack() as es:
        def sb(name, shape, dt):
            return es.enter_context(nc.sbuf_tensor(name + C.sfx, shape, dt))
        xc = sb("f_xc", [128, 8, 1024], F32)
        sq = sb("f_sq", [128, 8, 512], BF16)
        rstd = sb("f_rstd", [128, 1024], F32)
        hT = sb("f_hT", [128, 8, 1024], BF16)
        wg = [sb("f_wg%d" % i, [128, 8, 512], BF16) for i in range(2)]
        wu = [sb("f_wu%d" % i, [128, 8, 512], BF16) for i in range(2)]
        gT = sb("f_gT", [128, 22, 1024], BF16)
        wo = sb("f_wo", [128, 22, 1024], BF16)
        tmp = [sb("f_tmp%d" % i, [128, 512], F32) for i in range(2)]
        Bxc, Bsq, Bwo = P.buf("xc"), P.buf("sq"), P.buf("wo")
        Brstd = [P.buf("rstd") for _ in range(2)]
        BhT = [P.buf("hT") for _ in range(2)]
        Bw = [P.buf("w") for _ in range(2)]
        Bg = [P.buf("g") for _ in range(2)]
        Btmp = [P.buf("tmp") for _ in range(2)]
        pp = PsumPool(C, es, 8, "f_ps")

        P.dma(POOL, [(wo[:, 0:11, :], wo_d[:, 0:11, :]), (wo[:, 11:22, :], wo_d[:, 11:22, :])], Bwo, writes=[Bwo])
        wcnt = 0
        tcnt = 0
        for c in range(T // 1024):
            t0 = c * 1024
            P.dma(SP, [(xc[:, 0:4, :], xv[:, 0:4, t0:t0 + 1024]), (xc[:, 4:8, :], xv[:, 4:8, t0:t0 + 1024])],
                  Bxc, writes=[Bxc])
            for h in range(2):
                emit_norm(C, xc[:, :, hs(h)], Bxc, 512, gcol, sq[:], Bsq, rstd[:, hs(h)], Brstd[h],
                          hT[:, :, hs(h)], BhT[h], pp)
            for fg in range(6):
                nt = 4 if fg < 5 else 2
                ncol = nt * 128
                wbi = wcnt % 2
                wcnt += 1
                P.dma(POOL, [(wg[wbi][:, :, :ncol], wi[:, :, fg * 512:fg * 512 + ncol]),
                             (wu[wbi][:, :, :ncol], wi[:, :, FF + fg * 512:FF + fg * 512 + ncol])],
                      Bw[wbi], writes=[Bw[wbi]])
                for j in range(nt):
                    f = fg * 4 + j
                    for h in range(2):
                        psg, Bpg = pp.next()
                        psu, Bpu = pp.next()
                        for k in range(8):
                            P.add(PE, lambda e, k=k, j=j, h=h, psg=psg, wbi=wbi: e.matmul(
                                psg[:, :], wg[wbi][:, k, j * 128:(j + 1) * 128], hT[:, k, hs(h)],
                                start=(k == 0), stop=(k == 7)), reads=[Bw[wbi], BhT[h]], writes=[Bpg])
                        for k in range(8):
                            P.add(PE, lambda e, k=k, j=j, h=h, psu=psu, wbi=wbi: e.matmul(
                                psu[:, :], wu[wbi][:, k, j * 128:(j + 1) * 128], hT[:, k, hs(h)],
                                start=(k == 0), stop=(k == 7)), reads=[Bw[wbi], BhT[h]], writes=[Bpu])
                        ti = tcnt % 2
                        tcnt += 1
                        P.add(ACT, lambda e, psg=psg, ti=ti: e.activation(out=tmp[ti][:], in_=psg[:, :], func=AF.Silu),
                              reads=[Bpg], writes=[Btmp[ti]])
                        P.add(DVE, lambda e, psu=psu, ti=ti, f=f, h=h: e.tensor_tensor(
                            out=gT[:, f, hs(h)], in0=tmp[ti][:], in1=psu[:, :], op=ALU.mult),
                            reads=[Btmp[ti], Bpu], writes=[Bg[h]])
            for d in range(8):
                for h in range(2):
                    ps, Bp = pp.next()
                    for fc in range(22):
                        P.add(PE, lambda e, fc=fc, d=d, h=h, ps=ps: e.matmul(
                            ps[:, :], wo[:, fc, d * 128:(d + 1) * 128], gT[:, fc, hs(h)],
                            start=(fc == 0), stop=(fc == 21)), reads=[Bwo, Bg[h]], writes=[Bp])
                    P.add(DVE, lambda e, d=d, h=h, ps=ps: e.scalar_tensor_tensor(
                        out=xc[:, d, hs(h)], in0=ps[:, :], scalar=0.5, in1=xc[:, d, hs(h)],
                        op0=ALU.mult, op1=ALU.add), reads=[Bp, Bxc], writes=[Bxc])
            P.dma(SP, [(xv[:, 0:4, t0:t0 + 1024], xc[:, 0:4, :]), (xv[:, 4:8, t0:t0 + 1024], xc[:, 4:8, :])],
                  Bxc, reads=[Bxc])
        P.barrier()


def stage_inproj(C, l):
    nc, P = C.nc, C.P
    gcol = l * LV + V_MIX
    wi = wview(C.wb[("w_in", l)])
    xv = wview(C.xT)
    htv = wview(C.HT)
    with ExitStack() as es:
        def sb(name, shape, dt):
            return es.enter_context(nc.sbuf_tensor(name + C.sfx, shape, dt))
        w1 = sb("i_w1", [128, 8, 2304], BF16)
        xb = [sb("i_xb%d" % i, [128, 8, 512], F32) for i in range(2)]
        sq = sb("i_sq", [128, 8, 512], BF16)
        rstd = sb("i_rstd", [128, 512], F32)
        hT = [sb("i_hT%d" % i, [128, 8, 512], BF16) for i in range(2)]
        groups = [
            ("p", 0, 2, C.PT, F32, None),
            ("q", 2, 4, C.QT, BF16, 0.125),
            ("k", 6, 4, C.KT, BF16, None),
            ("lx", 14, 2, C.LXT, F32, None),
            ("lg", 16, 2, C.LGT, F32, None),
        ]
        stg = {g[0]: [sb("i_s%s%d" % (g[0], i), [128, g[2], 512], g[4]) for i in range(2)] for g in groups}
        stgv = [sb("i_sv%d" % i, [128, 4, 512], BF16) for i in range(2)]
        Bw1, Bsq, Brstd = P.buf(), P.buf(), P.buf()
        Bxb = [P.buf() for _ in range(2)]
        BhT = [P.buf() for _ in range(2)]
        Bstg = {g[0]: [P.buf() for _ in range(2)] for g in groups}
        Bstgv = [P.buf() for _ in range(2)]
        pp = PsumPool(C, es, 8, "i_ps")
        P.dma(POOL, [(w1[:, 0:4, :], wi[:, 0:4, 0:2304]), (w1[:, 4:8, :], wi[:, 4:8, 0:2304])], Bw1, writes=[Bw1])
        ev = 0
        for n in range(T // 512):
            t0 = n * 512
            pb = n % 2
            P.dma(SP, [(xb[pb][:], xv[:, :, t0:t0 + 512])], Bxb[pb], writes=[Bxb[pb]])
            emit_norm(C, xb[pb][:], Bxb[pb], 512, gcol, sq[:], Bsq, rstd[:], Brstd, hT[pb][:], BhT[pb], pp)
            P.dma(SP, [(htv[:, :, t0:t0 + 512], hT[pb][:])], BhT[pb], reads=[BhT[pb]])
            for (name, t1, ntile, dten, dt, scale) in groups:
                st = stg[name][pb]
                Bs = Bstg[name][pb]
                for j in range(ntile):
                    col = (t1 + j) * 128
                    ps, Bp = pp.next()
                    for k in range(8):
                        P.add(PE, lambda e, k=k, col=col, ps=ps, pb=pb: e.matmul(
                            ps[:, :], w1[:, k, col:col + 128], hT[pb][:, k, :], start=(k == 0), stop=(k == 7)),
                            reads=[Bw1, BhT[pb]], writes=[Bp])
                    if scale is not None:
                        P.add(ACT, lambda e, ps=ps, st=st, j=j, scale=scale: e.mul(st[:, j, :], ps[:, :], scale),
                              reads=[Bp], writes=[Bs])
                    elif ev % 2 == 0:
                        P.add(ACT, lambda e, ps=ps, st=st, j=j: e.copy(out=st[:, j, :], in_=ps[:, :]),
                              reads=[Bp], writes=[Bs])
                    else:
                        P.add(DVE, lambda e, ps=ps, st=st, j=j: e.tensor_copy(st[:, j, :], ps[:, :]),
                              reads=[Bp], writes=[Bs])
                    ev += 1
                dv = dten.rearrange("(j p) t -> p j t", p=128)
                P.dma(SP, [(dv[:, :, t0:t0 + 512], st[:])], Bs, reads=[Bs])
            for tt in range(4):
                ps, Bp = pp.next()
                for k in range(8):
                    P.add(PE, lambda e, k=k, tt=tt, ps=ps, pb=pb: e.matmul(
                        ps[:, :], hT[pb][:, k, tt * 128:(tt + 1) * 128], w1[:, k, 1280:1792],
                        start=(k == 0), stop=(k == 7)), reads=[Bw1, BhT[pb]], writes=[Bp])
                P.add(DVE, lambda e, ps=ps, tt=tt, pb=pb: e.tensor_copy(stgv[pb][:, tt, :], ps[:, :]),
                      reads=[Bp], writes=[Bstgv[pb]])
            vv = C.V.rearrange("(n p) e -> p n e", p=128)
            P.dma(SP, [(vv[:, n * 4:n * 4 + 4, :], stgv[pb][:])], Bstgv[pb], reads=[Bstgv[pb]])
        P.barrier()


def stage_pool(C, l):
    nc, P = C.nc, C.P
    PADL = 16
    WID = S + 32
    with ExitStack() as es:
        def sb(name, shape, dt):
            return es.enter_context(nc.sbuf_tensor(name + C.sfx, shape, dt))
        pp_ = sb("p_pp", [128, WID], F32)
        A = [sb("p_a%d" % i, [128, WID], F32) for i in range(2)]
        mix = sb("p_mix", [128, S], F32)
        mixb = sb("p_mixb", [128, S], BF16)
        invc = sb("p_invc", [128, S], F32)
        wbd = sb("p_wbd", [128, 128], BF16)
        wst = sb("p_wst", [128, 128], F32)
        outb = [sb("p_out%d" % i, [128, S], BF16) for i in range(2)]
        Bpp, Bmix, Bmixb, Binvc, Bwbd, Bwst = P.buf(), P.buf(), P.buf(), P.buf(), P.buf(), P.buf()
        BA = [P.buf() for _ in range(2)]
        Bout = [P.buf() for _ in range(2)]
        pp = PsumPool(C, es, 4, "p_ps")
        P.add(DVE, lambda e: e.memset(pp_[:], 0.0), writes=[Bpp])
        it = 0
        for j in range(2):
            P.dma(SP, [(invc[:], C.inp["invc"][j])], Binvc, writes=[Binvc])
            P.add(DVE, lambda e: e.memset(wst[:], 0.0), writes=[Bwst])
            P.dma(SP, [(wst[0:64, 0:64], C.inp["pool_w"][l, 2 * j]), (wst[64:128, 64:128], C.inp["pool_w"][l, 2 * j + 1])],
                  Bwst, writes=[Bwst])
            P.add(DVE, lambda e: e.tensor_copy(wbd[:], wst[:]), reads=[Bwst], writes=[Bwbd])
            for b in range(2):
                P.dma(SP, [(pp_[:, PADL:PADL + S], C.PT[j * 128:(j + 1) * 128, b * S:(b + 1) * S])], Bpp, writes=[Bpp])
                wins = POOL_WINDOWS[2 * j:2 * j + 2]
                src = pp_
                Bsrc = Bpp
                w = 1
                k = 0
                have = {}
                while w < max(wins):
                    dst = A[k % 2]
                    Bd = BA[k % 2]
                    P.add(DVE, lambda e, src=src, dst=dst, w=w: e.tensor_tensor(
                        out=dst[:, w:WID], in0=src[:, w:WID], in1=src[:, 0:WID - w], op=ALU.add),
                        reads=[Bsrc], writes=[Bd])
                    w *= 2
                    src, Bsrc = dst, Bd
                    k += 1
                    for hh, ww in enumerate(wins):
                        if ww == w:
                            o = PADL + w // 2 - 1
                            ps_ = slice(hh * 64, hh * 64 + 64)
                            P.add(DVE, lambda e, src=src, o=o, ps_=ps_: e.tensor_tensor(
                                out=mix[ps_, :], in0=src[ps_, o:o + S], in1=invc[ps_, :], op=ALU.mult),
                                reads=[Bsrc, Binvc], writes=[Bmix])
                P.add(DVE, lambda e: e.tensor_tensor(out=mixb[:], in0=mix[:], in1=pp_[:, PADL:PADL + S], op=ALU.subtract),
                      reads=[Bmix, Bpp], writes=[Bmixb])
                ob = outb[it % 2]
                Bo = Bout[it % 2]
                it += 1
                for n in range(S // 512):
                    ps, Bp = pp.next()
                    P.add(PE, lambda e, ps=ps, n=n: e.matmul(ps[:, :], wbd[:, :], mixb[:, n * 512:(n + 1) * 512],
                                                             start=True, stop=True), reads=[Bwbd, Bmixb], writes=[Bp])
                    vc = l * LV + V_PSCALE + j
                    P.add(ACT, lambda e, ps=ps, n=n, ob=ob, vc=vc: e.activation(
                        out=ob[:, n * 512:(n + 1) * 512], in_=ps[:, :], func=AF.Copy, scale=C.vecs[:, vc:vc + 1]),
                        reads=[Bp, C.Bconst], writes=[Bo])
                P.dma(SP, [(C.YP[j * 128:(j + 1) * 128, b * S:(b + 1) * S], ob[:])], Bo, reads=[Bo])
        P.barrier()


def stage_lru(C, l):
    nc, P = C.nc, C.P
    with ExitStack() as es:
        def sb(name, shape, dt):
            return es.enter_context(nc.sbuf_tensor(name + C.sfx, shape, dt))
        lx = sb("r_lx", [128, S + 4], F32)
        lg = sb("r_lg", [128, S], F32)
        xf = sb("r_xf", [128, S], F32)
        xfb = sb("r_xfb", [128, S], BF16)
        ra = sb("r_ra", [128, S], F32)
        iu = sb("r_iu", [128, S], F32)
        tm = sb("r_tm", [128, S], F32)
        hh = [sb("r_h%d" % i, [128, S], F32) for i in range(2)]
        yb = sb("r_yb", [128, S], BF16)
        wst = sb("r_wst", [128, 128], F32)
        wa = [sb("r_wa%d" % i, [128, 128], BF16) for i in range(2)]
        wx = [sb("r_wx%d" % i, [128, 128], BF16) for i in range(2)]
        sc = sb("r_sc", [128, 8], F32)
        Blx, Blg, Bxf, Bxfb, Bra, Biu, Btm, Byb, Bwst, Bsc = (P.buf() for _ in range(10))
        Bh = [P.buf() for _ in range(2)]
        Bwa = [P.buf() for _ in range(2)]
        Bwx = [P.buf() for _ in range(2)]
        pp = PsumPool(C, es, 8, "r_ps")
        P.add(DVE, lambda e: e.memset(lx[:], 0.0), writes=[Blx])
        vb = l * LV
        for j in range(2):
            for dr in range(2):
                lc = vb + V_LAM + dr * 2 + j
                P.add(ACT, lambda e, lc=lc, dr=dr: e.activation(
                    out=sc[:, 4 + dr:5 + dr], in_=C.vecs[:, lc:lc + 1], func=AF.Abs),
                    reads=[C.Bconst], writes=[Bsc])
                P.add(ACT, lambda e, dr=dr: e.activation(out=sc[:, 4 + dr:5 + dr], in_=sc[:, 4 + dr:5 + dr],
                                                         func=AF.Exp, scale=-1.0), reads=[Bsc], writes=[Bsc])
                P.add(ACT, lambda e, dr=dr: e.activation(out=sc[:, 4 + dr:5 + dr], in_=sc[:, 4 + dr:5 + dr],
                                                         func=AF.Ln, bias=C.epsn[:, 1:2]), reads=[Bsc, C.Bconst], writes=[Bsc])
                P.add(DVE, lambda e, lc=lc, dr=dr: e.tensor_scalar(
                    out=sc[:, 6 + dr:7 + dr], in0=C.vecs[:, lc:lc + 1], scalar1=-1.0, scalar2=0.0,
                    op0=ALU.mult, op1=ALU.max), reads=[C.Bconst, Bsc], writes=[Bsc])
                P.add(DVE, lambda e, dr=dr: e.tensor_tensor(out=sc[:, dr:dr + 1], in0=sc[:, 4 + dr:5 + dr],
                                                            in1=sc[:, 6 + dr:7 + dr], op=ALU.add), reads=[Bsc], writes=[Bsc])
                P.add(DVE, lambda e, dr=dr: e.tensor_scalar(out=sc[:, dr:dr + 1], in0=sc[:, dr:dr + 1], scalar1=-8.0,
                                                            scalar2=None, op0=ALU.mult), reads=[Bsc], writes=[Bsc])
                for (wsrc, wdst, Bwd) in ((C.inp["lru_w_a"], wa[dr], Bwa[dr]), (C.inp["lru_w_x"], wx[dr], Bwx[dr])):
                    P.add(DVE, lambda e: e.memset(wst[:], 0.0), writes=[Bwst])
                    P.dma(SP, [(wst[0:64, 0:64], wsrc[l, dr, 2 * j]), (wst[64:128, 64:128], wsrc[l, dr, 2 * j + 1])],
                          Bwst, writes=[Bwst])
                    P.add(DVE, lambda e, wdst=wdst: e.tensor_copy(wdst[:], wst[:]), reads=[Bwst], writes=[Bwd])
            for b in range(2):
                rows = slice(j * 128, (j + 1) * 128)
                cols = slice(b * S, (b + 1) * S)
                P.dma(SP, [(lx[:, 2:2 + S], C.LXT[rows, cols])], Blx, writes=[Blx])
                P.dma(SP, [(lg[:], C.LGT[rows, cols])], Blg, writes=[Blg])
                cw = vb + V_CONVW
                cbc = vb + V_CONVB + j
                P.add(DVE, lambda e, cw=cw, cbc=cbc, j=j: e.tensor_scalar(
                    out=xf[:], in0=lx[:, 0:S], scalar1=C.vecs[:, cw + j:cw + j + 1], scalar2=C.vecs[:, cbc:cbc + 1],
                    op0=ALU.mult, op1=ALU.add), reads=[Blx, C.Bconst], writes=[Bxf])
                for i in range(1, 4):
                    P.add(DVE, lambda e, i=i, cw=cw, j=j: e.scalar_tensor_tensor(
                        out=xf[:], in0=lx[:, i:i + S], scalar=C.vecs[:, cw + 2 * i + j:cw + 2 * i + j + 1], in1=xf[:],
                        op0=ALU.mult, op1=ALU.add), reads=[Blx, Bxf, C.Bconst], writes=[Bxf])
                P.add(ACT, lambda e: e.copy(out=xfb[:], in_=xf[:]), reads=[Bxf], writes=[Bxfb])
                for dr in range(2):
                    bac = vb + V_BA + dr * 2 + j
                    bxc = vb + V_BX + dr * 2 + j
                    for (wt, Bwt, dst, Bdst, bc) in ((wa[dr], Bwa[dr], ra, Bra, bac), (wx[dr], Bwx[dr], iu, Biu, bxc)):
                        for n in range(S // 512):
                            ps, Bp = pp.next()
                            P.add(PE, lambda e, ps=ps, n=n, wt=wt: e.matmul(
                                ps[:, :], wt[:, :], xfb[:, n * 512:(n + 1) * 512], start=True, stop=True),
                                reads=[Bwt, Bxfb], writes=[Bp])
                            P.add(ACT, lambda e, ps=ps, n=n, dst=dst, bc=bc: e.activation(
                                out=dst[:, n * 512:(n + 1) * 512], in_=ps[:, :], func=AF.Sigmoid,
                                bias=C.vecs[:, bc:bc + 1]), reads=[Bp, C.Bconst], writes=[Bdst])
                    P.add(ACT, lambda e, dr=dr: e.activation(out=ra[:], in_=ra[:], func=AF.Exp, scale=sc[:, dr:dr + 1]),
                          reads=[Bra, Bsc], writes=[Bra])
                    P.add(DVE, lambda e: e.tensor_tensor(out=tm[:], in0=ra[:], in1=ra[:], op=ALU.mult),
                          reads=[Bra], writes=[Btm])
                    P.add(ACT, lambda e: e.activation(out=tm[:], in_=tm[:], func=AF.Sqrt, scale=-1.0, bias=C.epsn[:, 1:2]),
                          reads=[Btm, C.Bconst], writes=[Btm])
                    P.add(DVE, lambda e: e.tensor_tensor(out=iu[:], in0=iu[:], in1=xf[:], op=ALU.mult),
                          reads=[Biu, Bxf], writes=[Biu])
                    P.add(DVE, lambda e: e.tensor_tensor(out=iu[:], in0=iu[:], in1=tm[:], op=ALU.mult),
                          reads=[Biu, Btm], writes=[Biu])
                    if dr == 0:
                        P.add(DVE, lambda e: e.tensor_tensor_scan(hh[0][:], ra[:], iu[:], 0.0, ALU.mult, ALU.add),
                              reads=[Bra, Biu], writes=[Bh[0]])
                    else:
                        def rev(t):
                            a = t[:]
                            pstep = a.ap[0][0]
                            return bass.AP(a.tensor, a.offset + (S - 1), [[pstep, 128], [-1, S]])
                        P.add(DVE, lambda e, rev=rev: e.tensor_tensor_scan(rev(hh[1]), rev(ra), rev(iu), 0.0, ALU.mult, ALU.add),
                              reads=[Bra, Biu], writes=[Bh[1]])
                P.add(DVE, lambda e: e.tensor_tensor(out=hh[0][:], in0=hh[0][:], in1=hh[1][:], op=ALU.add),
                      reads=[Bh[0], Bh[1]], writes=[Bh[0]])
                P.add(ACT, lambda e: e.activation(out=tm[:], in_=lg[:], func=AF.Gelu), reads=[Blg, Btm], writes=[Btm])
                P.add(DVE, lambda e: e.tensor_tensor(out=yb[:], in0=tm[:], in1=hh[0][:], op=ALU.mult),
                      reads=[Btm, Bh[0]], writes=[Byb])
                P.dma(SP, [(C.YL[rows, cols], yb[:])], Byb, reads=[Byb])
        P.barrier()


def stage_attn(C, l):
    nc, P = C.nc, C.P
    lam_init = 0.8 - 0.6 * math.exp(-0.3 * l)
    cfac = 1.0 - lam_init
    OOFF = (0, 160, 320)
    with ExitStack() as es:
        def sb(name, shape, dt):
            return es.enter_context(nc.sbuf_tensor(name + C.sfx, shape, dt))
        NSET = 2
        KA = [[sb("a_ka%d_%d" % (s_, m), [69, S], BF16) for m in range(2)] for s_ in range(NSET)]
        QA = [[[sb("a_qa%d_%d_%d" % (s_, m, sg), [69, S], BF16) for sg in range(2)] for m in range(2)] for s_ in range(NSET)]
        VP = [sb("a_vp%d" % s_, [128, 32, 129], BF16) for s_ in range(NSET)]
        BKA = [P.buf() for _ in range(NSET)]
        BQA = [P.buf() for _ in range(NSET)]
        BVP = [P.buf() for _ in range(NSET)]
        NPT = 3
        PT = [sb("a_pt%d" % i, [128, 1024], BF16) for i in range(NPT)]
        BPT = [P.buf() for _ in range(NPT)]
        lamt = sb("a_lam", [128, 256], F32)
        lams = sb("a_lams", [128, 8], F32)
        sg_t = sb("a_sg", [128, 128], F32)
        zz = sb("a_zz", [128, 16], F32)
        o_t = [sb("a_o%d" % i, [128, 128], F32) for i in range(2)]
        junk = sb("a_junk", [128, 128], F32)
        ob = [sb("a_ob%d" % i, [128, 128], BF16) for i in range(2)]
        ost = [sb("a_ost%d" % i, [128, 512], BF16) for i in range(2)]
        Blam, Bsg, Bzz, Bjunk = P.buf(), P.buf(), P.buf(), P.buf()
        Bo = [P.buf() for _ in range(2)]
        Bob = [P.buf() for _ in range(2)]
        Bost = [P.buf() for _ in range(2)]
        SP_ = [es.enter_context(nc.psum_tensor("a_ps%d%s" % (i, C.sfx), [128, 1024], F32)) for i in range(2)]
        BS = [P.buf() for _ in range(2)]
        OP_ = [es.enter_context(nc.psum_tensor("a_po%d%s" % (i, C.sfx), [128, 512], F32)) for i in range(3)]
        BO = [P.buf() for _ in range(3)]
        TP_ = es.enter_context(nc.psum_tensor("a_ptr" + C.sfx, [128, 1024], BF16))
        BT = P.buf()

        lsrc = C.inp["attn_lambda"][l]
        lap = bass.AP(lsrc.tensor, lsrc.offset, [[0, 128], [1, 256]])
        P.dma(SP, [(lamt[:], lap)], Blam, writes=[Blam])
        P.add(DVE, lambda e: e.memset(lams[:], 0.0), writes=[Blam])
        for i in range(2):
            P.add(DVE, lambda e, i=i: e.tensor_tensor(out=lamt[:, i * 128:i * 128 + 64], in0=lamt[:, i * 128:i * 128 + 64],
                                                      in1=lamt[:, i * 128 + 64:i * 128 + 128], op=ALU.mult),
                  reads=[Blam], writes=[Blam])
            P.add(DVE, lambda e, i=i: e.reduce_sum(out=lams[:, i:i + 1], in_=lamt[:, i * 128:i * 128 + 64],
                                                   axis=mybir.AxisListType.X), reads=[Blam], writes=[Blam])
        P.add(ACT, lambda e: e.activation(out=lams[:, 0:2], in_=lams[:, 0:2], func=AF.Exp), reads=[Blam], writes=[Blam])
        P.add(DVE, lambda e: e.tensor_tensor(out=lams[:, 2:3], in0=lams[:, 0:1], in1=lams[:, 1:2], op=ALU.subtract),
              reads=[Blam], writes=[Blam])
        P.add(DVE, lambda e: e.tensor_scalar(out=lams[:, 3:4], in0=lams[:, 2:3], scalar1=lam_init, scalar2=-1.0,
                                             op0=ALU.add, op1=ALU.mult), reads=[Blam], writes=[Blam])
        ssrc = C.inp["attn_subln"][l]
        sap = bass.AP(ssrc.tensor, ssrc.offset, [[0, 128], [1, 128]])
        P.dma(SP, [(sg_t[:], sap)], Bsg, writes=[Bsg])
        for s_ in range(NSET):
            P.add(DVE, lambda e, s_=s_: e.memset(VP[s_][:, :, 128:129], 1.0), writes=[BVP[s_]])
            P.dma(SP, [(KA[s_][m][64:69, :], C.inp["kaug"][:, :]) for m in range(2)], BKA[s_], writes=[BKA[s_]])

        def load_set(b, h, s_):
            cols = slice(b * S, (b + 1) * S)
            prs = []
            for m in range(2):
                r0 = h * 128 + m * 64
                prs.append((KA[s_][m][0:64, :], C.KT[r0:r0 + 64, cols]))
            P.dma(SP, prs, BKA[s_], writes=[BKA[s_]])
            prs = []
            for m in range(2):
                r0 = h * 128 + m * 64
                for sg in range(2):
                    prs.append((QA[s_][m][sg][0:64, :], C.QT[r0:r0 + 64, cols]))
                    prs.append((QA[s_][m][sg][64:69, :], C.inp["qaug"][h, sg]))
            P.dma(SP, prs, BQA[s_], writes=[BQA[s_]])
            vv = C.V.rearrange("(n p) e -> p n e", p=128)
            P.dma(SP, [(VP[s_][:, 8 * i:8 * i + 8, 0:128], vv[:, b * 32 + 8 * i:b * 32 + 8 * i + 8, h * 128:(h + 1) * 128])
                       for i in range(4)], BVP[s_], writes=[BVP[s_]])

        bh_list = [(b, h) for b in range(2) for h in range(4)][:DBG.get('bh', 8)]
        load_set(0, 0, 0)
        it = 0
        oc = 0
        for bi, (b, h) in enumerate(bh_list):
            s_ = bi % NSET
            if bi + 1 < len(bh_list):
                load_set(bh_list[bi + 1][0], bh_list[bi + 1][1], (bi + 1) % NSET)
            for qb in range(DBG.get('qb', 8)):
                q0 = qb * 512
                for kt in range(32):
                    k0 = kt * 128
                    si = it % 2
                    pi = it % NPT
                    it += 1
                    Sp = SP_[si]
                    jd = kt - 4 * qb
                    for m in range(2):
                        base = m * 512
                        ka = KA[s_][m][0:69, k0:k0 + 128]
                        if jd < 0 or jd > 3:
                            sg = 0 if jd < 0 else 1
                            P.add(PE, lambda e, Sp=Sp, ka=ka, m=m, sg=sg, base=base, s_=s_, q0=q0: e.matmul(
                                Sp[:, base:base + 512], ka, QA[s_][m][sg][0:69, q0:q0 + 512], start=True, stop=True,
                                skip_group_check=True), reads=[BKA[s_], BQA[s_]], writes=[BS[si]])
                        else:
                            first = True
                            if jd > 0:
                                P.add(PE, lambda e, Sp=Sp, ka=ka, m=m, base=base, s_=s_, q0=q0, jd=jd: e.matmul(
                                    Sp[:, base:base + 128 * jd], ka, QA[s_][m][1][0:69, q0:q0 + 128 * jd],
                                    start=True, stop=True, skip_group_check=True),
                                    reads=[BKA[s_], BQA[s_]], writes=[BS[si]])
                                first = False
                            P.add(PE, lambda e, Sp=Sp, ka=ka, m=m, base=base, s_=s_, q0=q0, jd=jd, first=first: e.matmul(
                                Sp[:, base + 128 * jd:base + 512], ka, QA[s_][m][0][0:69, q0 + 128 * jd:q0 + 512],
                                start=first, stop=False, skip_group_check=True),
                                reads=[BKA[s_], BQA[s_]], writes=[BS[si]])
                            P.add(PE, lambda e, Sp=Sp, base=base, jd=jd, h=h: e.matmul(
                                Sp[:, base + 128 * jd:base + 128 * jd + 128], C.idS[:, h, :], C.c0[:, :],
                                start=False, stop=True, skip_group_check=True),
                                reads=[C.Bconst], writes=[BS[si]])
                    P.add(ACT, lambda e, Sp=Sp, pi=pi: e.activation(out=PT[pi][:], in_=Sp[:, :], func=AF.Exp),
                          reads=[BS[si]], writes=[BPT[pi]])
                    for m in range(2):
                        for qt in range(4):
                            a = m * 4 + qt
                            bank, slot = a // 3, a % 3
                            oo = OOFF[slot]
                            P.add(PE, lambda e, pi=pi, m=m, qt=qt, bank=bank, oo=oo, slot=slot, kt=kt, s_=s_: e.matmul(
                                OP_[bank][:, oo:oo + 129], PT[pi][:, m * 512 + qt * 128:m * 512 + qt * 128 + 128],
                                VP[s_][:, kt, :], start=(kt == 0 and slot == 0), stop=(kt == 31),
                                skip_group_check=True), reads=[BPT[pi], BVP[s_]], writes=[BO[bank]])
                for bank in range(3):
                    n = 3 if bank < 2 else 2
                    zsrc = bass.AP(OP_[bank][:].tensor, OP_[bank][:, 128:129].offset, [[OP_[bank][:].ap[0][0], 128], [160, n]])
                    P.add(DVE, lambda e, zsrc=zsrc, bank=bank, n=n: e.reciprocal(zz[:, bank * 3:bank * 3 + n], zsrc),
                          reads=[BO[bank]], writes=[Bzz])
                P.add(DVE, lambda e: e.tensor_scalar(out=zz[:, 4:8], in0=zz[:, 4:8], scalar1=lams[:, 3:4], scalar2=None,
                                                     op0=ALU.mult), reads=[Bzz, Blam], writes=[Bzz])
                osi = (oc // 4) % 2
                for qt in range(4):
                    oi = oc % 2
                    oc += 1
                    a1, a2 = qt, 4 + qt
                    b1, s1 = a1 // 3, a1 % 3
                    b2, s2 = a2 // 3, a2 % 3
                    P.add(DVE, lambda e, oi=oi, b1=b1, s1=s1, a1=a1: e.tensor_scalar(
                        out=o_t[oi][:], in0=OP_[b1][:, OOFF[s1]:OOFF[s1] + 128], scalar1=zz[:, a1:a1 + 1], scalar2=None,
                        op0=ALU.mult), reads=[BO[b1], Bzz], writes=[Bo[oi]])
                    P.add(DVE, lambda e, oi=oi, b2=b2, s2=s2, a2=a2: e.scalar_tensor_tensor(
                        out=o_t[oi][:], in0=OP_[b2][:, OOFF[s2]:OOFF[s2] + 128], scalar=zz[:, a2:a2 + 1], in1=o_t[oi][:],
                        op0=ALU.mult, op1=ALU.add), reads=[BO[b2], Bzz, Bo[oi]], writes=[Bo[oi]])
                    P.add(DVE, lambda e, oi=oi: e.tensor_tensor(out=junk[:], in0=o_t[oi][:], in1=o_t[oi][:], op=ALU.mult),
                          reads=[Bo[oi]], writes=[Bjunk])
                    P.add(DVE, lambda e, qt=qt: e.reduce_sum(out=zz[:, 8 + qt:9 + qt], in_=junk[:], axis=mybir.AxisListType.X),
                          reads=[Bjunk], writes=[Bzz])
                    P.add(ACT, lambda e, qt=qt: e.activation(out=zz[:, 12 + qt:13 + qt], in_=zz[:, 8 + qt:9 + qt], func=AF.Sqrt,
                                                             scale=1.0 / (128.0 * cfac * cfac), bias=C.epsn[:, 2 + l:3 + l]),
                          reads=[Bzz, C.Bconst], writes=[Bzz])
                    P.add(DVE, lambda e, qt=qt: e.reciprocal(zz[:, 12 + qt:13 + qt], zz[:, 12 + qt:13 + qt]),
                          reads=[Bzz], writes=[Bzz])
                    P.add(DVE, lambda e, oi=oi, qt=qt: e.scalar_tensor_tensor(
                        out=ob[oi][:], in0=o_t[oi][:], scalar=zz[:, 12 + qt:13 + qt], in1=sg_t[:], op0=ALU.mult, op1=ALU.mult),
                        reads=[Bo[oi], Bzz, Bsg], writes=[Bob[oi]])
                    P.add(PE, lambda e, oi=oi, qt=qt: e.transpose(TP_[:, qt * 128:(qt + 1) * 128], ob[oi][:], C.ident[:, :]),
                          reads=[Bob[oi], C.Bconst], writes=[BT])
                P.add(DVE, lambda e, osi=osi: e.tensor_copy(ost[osi][:], TP_[:, 0:512]), reads=[BT], writes=[Bost[osi]])
                P.dma(SP, [(C.YA[h * 128:(h + 1) * 128, b * S + q0:b * S + q0 + 512], ost[osi][:])], Bost[osi], reads=[Bost[osi]])
        P.barrier()


def stage_merge(C, l):
    nc, P = C.nc, C.P
    wi = wview(C.wb[("w_in", l)])
    xv = wview(C.xT)
    htv = wview(C.HT)
    with ExitStack() as es:
        def sb(name, shape, dt):
            return es.enter_context(nc.sbuf_tensor(name + C.sfx, shape, dt))
        wgt = sb("m_wg", [128, 8, 3072], BF16)
        wbr = [sb("m_wbp", [128, 2, 1024], BF16), sb("m_wba", [128, 4, 1024], BF16), sb("m_wbl", [128, 2, 1024], BF16)]
        wot = sb("m_wo", [128, 8, 1024], BF16)
        nbr = (2, 4, 2)
        ysrc = (C.YP, C.YA, C.YL)
        hT = [sb("m_hT%d" % i, [128, 8, 512], BF16) for i in range(2)]
        yin = [[sb("m_y%d_%d" % (j, i), [128, nbr[j], 512], BF16) for i in range(2)] for j in range(3)]
        xb = [sb("m_xb%d" % i, [128, 8, 512], F32) for i in range(2)]
        mT = sb("m_mT", [128, 8, 512], BF16)
        gt = [sb("m_g%d" % j, [128, 512], F32) for j in range(3)]
        tt = [sb("m_t%d" % j, [128, 512], F32) for j in range(3)]
        Bw = P.buf()
        BhT = [P.buf() for _ in range(2)]
        Byin = [P.buf() for _ in range(2)]
        Bxb = [P.buf() for _ in range(2)]
        BmT = P.buf()
        Bg = [P.buf() for _ in range(3)]
        Bt = [P.buf() for _ in range(3)]
        pp = PsumPool(C, es, 8, "m_ps")
        P.dma(POOL, [(wgt[:, 0:4, :], wi[:, 0:4, 2304:5376]), (wgt[:, 4:8, :], wi[:, 4:8, 2304:5376]),
                     (wbr[0][:], wview(C.wb[("w_branch_pool", l)])), (wbr[1][:], wview(C.wb[("w_branch_attn", l)])),
                     (wbr[2][:], wview(C.wb[("w_branch_lru", l)])), (wot[:], wview(C.wb[("w_out", l)]))], Bw, writes=[Bw])
        for n in range(T // 512):
            t0 = n * 512
            pb = n % 2
            P.dma(SP, [(hT[pb][:], htv[:, :, t0:t0 + 512])], BhT[pb], writes=[BhT[pb]])
            P.dma(SP, [(yin[j][pb][:], ysrc[j].rearrange("(c p) t -> p c t", p=128)[:, :, t0:t0 + 512]) for j in range(3)],
                  Byin[pb], writes=[Byin[pb]])
            P.dma(SP, [(xb[pb][:], xv[:, :, t0:t0 + 512])], Bxb[pb], writes=[Bxb[pb]])
            for d in range(8):
                for j in range(3):
                    psg, Bpg = pp.next()
                    for k in range(8):
                        P.add(PE, lambda e, k=k, j=j, d=d, psg=psg, pb=pb: e.matmul(
                            psg[:, :], wgt[:, k, j * 1024 + d * 128:j * 1024 + d * 128 + 128], hT[pb][:, k, :],
                            start=(k == 0), stop=(k == 7)), reads=[Bw, BhT[pb]], writes=[Bpg])
                    mbc = l * LV + V_MB + j * 8 + d
                    P.add(ACT, lambda e, psg=psg, j=j, mbc=mbc: e.activation(
                        out=gt[j][:], in_=psg[:, :], func=AF.Sigmoid, bias=C.vecs[:, mbc:mbc + 1]),
                        reads=[Bpg, C.Bconst], writes=[Bg[j]])
                    psy, Bpy = pp.next()
                    for k in range(nbr[j]):
                        P.add(PE, lambda e, k=k, j=j, d=d, psy=psy, pb=pb: e.matmul(
                            psy[:, :], wbr[j][:, k, d * 128:(d + 1) * 128], yin[j][pb][:, k, :],
                            start=(k == 0), stop=(k == nbr[j] - 1)), reads=[Bw, Byin[pb]], writes=[Bpy])
                    P.add(DVE, lambda e, psy=psy, j=j: e.tensor_tensor(out=tt[j][:], in0=gt[j][:], in1=psy[:, :], op=ALU.mult),
                          reads=[Bg[j], Bpy], writes=[Bt[j]])
                P.add(DVE, lambda e: e.tensor_tensor(out=tt[0][:], in0=tt[0][:], in1=tt[1][:], op=ALU.add),
                      reads=[Bt[0], Bt[1]], writes=[Bt[0]])
                P.add(DVE, lambda e, d=d: e.tensor_tensor(out=mT[:, d, :], in0=tt[0][:], in1=tt[2][:], op=ALU.add),
                      reads=[Bt[0], Bt[2]], writes=[BmT])
            for d in range(8):
                ps, Bp = pp.next()
                for k in range(8):
                    P.add(PE, lambda e, k=k, d=d, ps=ps: e.matmul(
                        ps[:, :], wot[:, k, d * 128:(d + 1) * 128], mT[:, k, :], start=(k == 0), stop=(k == 7)),
                        reads=[Bw, BmT], writes=[Bp])
                P.add(DVE, lambda e, d=d, ps=ps, pb=pb: e.tensor_tensor(
                    out=xb[pb][:, d, :], in0=ps[:, :], in1=xb[pb][:, d, :], op=ALU.add),
                    reads=[Bp, Bxb[pb]], writes=[Bxb[pb]])
            P.dma(SP, [(xv[:, :, t0:t0 + 512], xb[pb][:])], Bxb[pb], reads=[Bxb[pb]])
        P.barrier()


def stage_final(C):
    nc, P = C.nc, C.P
    xv = wview(C.xT)
    ov = wview(C.out)
    with ExitStack() as es:
        def sb(name, shape, dt):
            return es.enter_context(nc.sbuf_tensor(name + C.sfx, shape, dt))
        xb = [sb("z_xb%d" % i, [128, 8, 512], F32) for i in range(2)]
        yo = [sb("z_yo%d" % i, [128, 8, 512], F32) for i in range(2)]
        sq = sb("z_sq", [128, 8, 512], BF16)
        rstd = sb("z_rstd", [128, 512], F32)
        Bxb = [P.buf() for _ in range(2)]
        Byo = [P.buf() for _ in range(2)]
        Bsq, Brstd = P.buf(), P.buf()
        pp = PsumPool(C, es, 2, "z_ps")
        for n in range(T // 512):
            t0 = n * 512
            pb = n % 2
            P.dma(SP, [(xb[pb][:], xv[:, :, t0:t0 + 512])], Bxb[pb], writes=[Bxb[pb]])
            emit_norm(C, xb[pb][:], Bxb[pb], 512, V_FINAL, sq[:], Bsq, rstd[:], Brstd, yo[pb][:], Byo[pb], pp)
            P.dma(SP, [(ov[:, :, t0:t0 + 512], yo[pb][:])], Byo[pb], reads=[Byo[pb]])
        P.barrier()


INPUT_SPECS = [
    ("ffn1_w_in", (2, 1024, 5632)), ("ffn1_w_out", (2, 2816, 1024)), ("w_in", (2, 1024, 5376)),
    ("pool_w", (2, 4, 64, 64)), ("attn_lambda", (2, 4, 64)), ("attn_subln", (2, 128)),
    ("lru_w_a", (2, 2, 4, 64, 64)), ("lru_w_x", (2, 2, 4, 64, 64)),
    ("w_branch_pool", (2, 256, 1024)), ("w_branch_attn", (2, 512, 1024)), ("w_branch_lru", (2, 256, 1024)),
    ("w_out", (2, 1024, 1024)), ("ffn2_w_in", (2, 1024, 5632)), ("ffn2_w_out", (2, 2816, 1024)),
]


def build(stages=None, dbg=(), feed=()):
    nc = bass.Bass("TRN2", target_bir_lowering=False)
    C = Ctx()
    C.nc = nc
    C.P = Prog(nc)
    C.inp = {}
    for name, shape in INPUT_SPECS:
        C.inp[name] = nc.dram_tensor(name, list(shape), F32, kind="ExternalInput").ap()
    C.inp["vecs"] = nc.dram_tensor("vecs", [128, NV], F32, kind="ExternalInput").ap()
    C.inp["invc"] = nc.dram_tensor("invc", [2, 128, S], F32, kind="ExternalInput").ap()
    C.inp["kaug"] = nc.dram_tensor("kaug", [5, S], BF16, kind="ExternalInput").ap()
    C.inp["qaug"] = nc.dram_tensor("qaug", [4, 2, 5, S], BF16, kind="ExternalInput").ap()
    C.inp["cmat"] = nc.dram_tensor("cmat", [128, 7, 128], BF16, kind="ExternalInput").ap()
    dbg = set(dbg)

    def scratch(name, shape, dt):
        kind = "ExternalOutput" if name in dbg else ("ExternalInput" if name in feed else "Internal")
        return nc.dram_tensor(name, list(shape), dt, kind=kind).ap()

    C.xin = nc.dram_tensor("xin", [D, T], F32, kind="ExternalInput").ap()
    C.xT = scratch("xT", (D, T), F32)
    C.out = nc.dram_tensor("out", [D, T], F32, kind="ExternalOutput").ap()
    C.HT = scratch("HT", (D, T), BF16)
    C.PT = scratch("PT", (256, T), F32)
    C.QT = scratch("QT", (512, T), BF16)
    C.KT = scratch("KT", (512, T), BF16)
    C.V = scratch("V", (T, 512), BF16)
    C.LXT = scratch("LXT", (256, T), F32)
    C.LGT = scratch("LGT", (256, T), F32)
    C.YP = scratch("YP", (256, T), BF16)
    C.YA = scratch("YA", (512, T), BF16)
    C.YL = scratch("YL", (256, T), BF16)
    C.wb = {}
    for l in range(DEPTH):
        for name, shape in INPUT_SPECS:
            if name.startswith("w_") or name.startswith("ffn"):
                C.wb[(name, l)] = nc.dram_tensor("wb_%s_%d" % (name, l), list(shape[1:]), BF16).ap()

    with ExitStack() as es:
        P = C.P
        C.vecs = es.enter_context(nc.sbuf_tensor("c_vecs", [128, NV], F32))
        cm = es.enter_context(nc.sbuf_tensor("c_cmat", [128, 7, 128], BF16))
        C.epsn = es.enter_context(nc.sbuf_tensor("c_eps", [128, 4], F32))
        C.ones = cm[:, 0, :]
        C.ident = cm[:, 1, :]
        C.c0 = cm[:, 2, :]
        C.idS = cm[:, 3:7, :]
        C.Bconst = Buf("const")
        P.dma(SP, [(C.vecs[:], C.inp["vecs"]), (cm[:], C.inp["cmat"])], C.Bconst, writes=[C.Bconst])
        P.add(DVE, lambda e: e.memset(C.epsn[:, 0:1], NORM_EPS), writes=[C.Bconst])
        P.add(DVE, lambda e: e.memset(C.epsn[:, 1:2], 1.0), writes=[C.Bconst])
        for l in range(DEPTH):
            cf = 1.0 - (0.8 - 0.6 * math.exp(-0.3 * l))
            P.add(DVE, lambda e, l=l, cf=cf: e.memset(C.epsn[:, 2 + l:3 + l], SUBLN_EPS / (cf * cf)), writes=[C.Bconst])
        if C.xin is not None:
            Bcp = P.buf("xcopy")
            P.dma(SP, [(C.xT[r:r + 128, :], C.xin[r:r + 128, :]) for r in range(0, D, 128)], Bcp, writes=[Bcp])
        P.barrier()

        def on(tag):
            C.sfx = "_" + tag
            return stages is None or tag in stages
        if on("W"):
            stage_weights(C)
        for l in range(DEPTH):
            if on("F1_%d" % l):
                stage_ffn(C, l, 1)
            if on("I_%d" % l):
                stage_inproj(C, l)
            if on("P_%d" % l):
                stage_pool(C, l)
            if on("R_%d" % l):
                stage_lru(C, l)
            if on("A_%d" % l):
                stage_attn(C, l)
            if on("M_%d" % l):
                stage_merge(C, l)
            if on("F2_%d" % l):
                stage_ffn(C, l, 2)
        if on("Z"):
            stage_final(C)
        P.emit(es)
    return nc, C


def chunkvec(v):
    v = np.asarray(v, np.float32)
    return np.ascontiguousarray(v.reshape(-1, 128).T)


def host_tables(inputs):
    vecs = np.zeros((128, NV), np.float32)
    for l in range(DEPTH):
        b = l * LV
        vecs[:, b + V_FFN1:b + V_FFN1 + 8] = chunkvec(inputs["ffn1_norm"][l])
        vecs[:, b + V_MIX:b + V_MIX + 8] = chunkvec(inputs["mix_norm"][l])
        vecs[:, b + V_FFN2:b + V_FFN2 + 8] = chunkvec(inputs["ffn2_norm"][l])
        vecs[:, b + V_PSCALE:b + V_PSCALE + 2] = chunkvec(inputs["pool_scale"][l])
        for i in range(4):
            vecs[:, b + V_CONVW + 2 * i:b + V_CONVW + 2 * i + 2] = chunkvec(inputs["lru_conv_w"][l, i])
        vecs[:, b + V_CONVB:b + V_CONVB + 2] = chunkvec(inputs["lru_conv_b"][l])
        for dr in range(2):
            vecs[:, b + V_BA + 2 * dr:b + V_BA + 2 * dr + 2] = chunkvec(inputs["lru_b_a"][l, dr])
            vecs[:, b + V_BX + 2 * dr:b + V_BX + 2 * dr + 2] = chunkvec(inputs["lru_b_x"][l, dr])
            vecs[:, b + V_LAM + 2 * dr:b + V_LAM + 2 * dr + 2] = chunkvec(inputs["lru_lambda"][l, dr])
        for j in range(3):
            vecs[:, b + V_MB + 8 * j:b + V_MB + 8 * j + 8] = chunkvec(inputs["merge_bias"][l, j])
    vecs[:, V_FINAL:V_FINAL + 8] = chunkvec(inputs["final_norm"])
    t = np.arange(S)
    invc = np.zeros((2, 128, S), np.float32)
    for g, w in enumerate(POOL_WINDOWS):
        lo = np.clip(t - w // 2, 0, S)
        hi = np.clip(t + w - w // 2, 0, S)
        invc[g // 2, (g % 2) * 64:(g % 2) * 64 + 64, :] = (1.0 / (hi - lo).astype(np.float32))[None, :]
    kaug = np.zeros((5, S), np.float32)
    kaug[0:3] = 1.0
    kaug[3] = 128 * (t // 128)
    kaug[4] = t % 128
    qaug = np.zeros((4, 2, 5, S), np.float32)
    c = t % 512
    for h in range(4):
        for sg, sigma in enumerate((-1.0, 1.0)):
            sl = SLOPES[h]
            qaug[h, sg, 0] = sigma * sl * (512 * (t // 512))
            qaug[h, sg, 1] = sigma * sl * (c % 256)
            qaug[h, sg, 2] = sigma * sl * (256 * (c // 256))
            qaug[h, sg, 3] = -sigma * sl
            qaug[h, sg, 4] = -sigma * sl
    cmat = np.zeros((128, 7, 128), np.float32)
    cmat[:, 0, :] = 1.0
    cmat[:, 1, :] = np.eye(128)
    i = np.arange(128)
    cmat[:, 2, :] = -2.0 * np.maximum(i[:, None] - i[None, :], 0)
    for h in range(4):
        cmat[:, 3 + h, :] = SLOPES[h] * np.eye(128)
    bf = ml_dtypes.bfloat16
    return {"vecs": vecs, "invc": invc, "kaug": kaug.astype(bf), "qaug": qaug.astype(bf), "cmat": cmat.astype(bf)}


def make_in_maps(inputs, ncores=NCORES, xname="xin"):
    tabs = host_tables(inputs)
    shared = {name: np.ascontiguousarray(np.asarray(inputs[name], np.float32)) for name, _ in INPUT_SPECS}
    shared.update(tabs)
    x = np.asarray(inputs["x"], np.float32)
    maps = []
    for c in range(ncores):
        m = dict(shared)
        m[xname] = np.ascontiguousarray(x[2 * c:2 * c + 2].reshape(T, D).T)
        maps.append(m)
    return maps


_CACHE = {}


def kernel(**inputs):
    if "nc" not in _CACHE:
        _CACHE["nc"] = build()[0]
    nc = _CACHE["nc"]
    in_maps = make_in_maps(inputs)
    res = run_bass_kernel_spmd(nc, in_maps, core_ids=list(range(NCORES)))
    outs = []
    for c in range(NCORES):
        o = np.asarray(res.results[c]["out"], np.float32)
        outs.append(o.T.reshape(2, S, D))
    return np.ascontiguousarray(np.concatenate(outs, axis=0))
```

```python
import math
from contextlib import ExitStack

import numpy as np
import ml_dtypes

import concourse.bass as bass
import concourse.mybir as mybir
from concourse.bass_utils import run_bass_kernel_spmd

F32 = mybir.dt.float32
BF16 = mybir.dt.bfloat16
AF = mybir.ActivationFunctionType
ALU = mybir.AluOpType

NCORES = 8
D = 1024
FF = 2816
S = 4096
T = 2 * S
INW = 5376
DEPTH = 2
NORM_EPS = 1e-6
SUBLN_EPS = 1e-5
SLOPES = [2.0 ** (-2.0 * (i + 1)) for i in range(4)]
POOL_WINDOWS = (2, 4, 8, 16)

DBG = {}
PE, ACT, DVE, POOL, SP = "pe", "act", "dve", "pool", "sp"
ENGS = (PE, ACT, DVE, POOL, SP)
SEM_EPOCH = 12000
DSEM_RETIRE = 12000

LV = 72
V_FFN1, V_MIX, V_FFN2, V_PSCALE, V_CONVW, V_CONVB, V_BA, V_BX, V_LAM, V_MB = 0, 8, 16, 24, 26, 34, 36, 40, 44, 48
V_FINAL = 2 * LV
NV = 2 * LV + 8


class DSem:
    __slots__ = ("sem", "count", "eng")

    def __init__(self):
        self.sem = None
        self.count = 0


class Buf:
    __slots__ = ("name", "writer", "readers", "dsem")

    def __init__(self, name=""):
        self.name = name
        self.writer = None
        self.readers = []
        self.dsem = None


class Op:
    __slots__ = ("eng", "fn", "deps", "signal", "sig", "is_dma")

    def __init__(self, eng, fn, is_dma=False):
        self.eng = eng
        self.fn = fn
        self.deps = []
        self.signal = False
        self.sig = None
        self.is_dma = is_dma


class Prog:
    def __init__(self, nc):
        self.nc = nc
        self.ops = {e: [] for e in ENGS}
        self.stage_bufs = []
        self.free_dsems = {}
        self.all_dsems = []
        self.nsem = 0

    def buf(self, name=""):
        b = Buf(name)
        self.stage_bufs.append(b)
        return b

    def _dsem(self, b, eng):
        if b.dsem is None:
            fl = self.free_dsems.setdefault(eng, [])
            if fl:
                b.dsem = fl.pop()
            else:
                b.dsem = DSem()
                b.dsem.eng = eng
                self.all_dsems.append(b.dsem)
        assert b.dsem.eng == eng, "DMA semaphore shared across issuing queues"
        return b.dsem

    def _track(self, op, reads, writes):
        deps = op.deps
        for b in reads:
            if b.writer is not None:
                deps.append(b.writer)
        for b in writes:
            if b.writer is not None:
                deps.append(b.writer)
            deps.extend(b.readers)
        for b in writes:
            b.writer = op
            b.readers = []
        for b in reads:
            r = b.readers
            if r and (not op.is_dma) and (not r[-1].is_dma) and r[-1].eng == op.eng:
                r[-1] = op
            else:
                r.append(op)
        self.ops[op.eng].append(op)
        return op

    def add(self, eng, fn, reads=(), writes=()):
        return self._track(Op(eng, fn), reads, writes)

    def dma(self, eng, pairs, owner, reads=(), writes=()):
        ds = self._dsem(owner, eng)
        op = Op(eng, None, is_dma=True)
        ds.count += 16 * len(pairs)
        op.sig = (ds, ds.count)
        op.signal = True

        def fn(e, pairs=pairs, ds=ds):
            for (o, i) in pairs:
                e.dma_start(out=o, in_=i).then_inc(ds.sem, 16)
        op.fn = fn
        return self._track(op, reads, writes)

    def barrier(self, engines=ENGS):
        alld = []
        for b in self.stage_bufs:
            if b.writer is not None:
                alld.append(b.writer)
            alld.extend(b.readers)
        for e in engines:
            op = Op(e, lambda en: en.nop())
            op.deps = list(alld)
            self.ops[e].append(op)
        for b in self.stage_bufs:
            if b.dsem is not None:
                if b.dsem.count < DSEM_RETIRE:
                    self.free_dsems.setdefault(b.dsem.eng, []).append(b.dsem)
                b.dsem = None
        self.stage_bufs = []

    def emit(self, es):
        nc = self.nc

        def newsem():
            s = es.enter_context(nc.semaphore("s%d" % self.nsem))
            self.nsem += 1
            return s

        for e in ENGS:
            for op in self.ops[e]:
                for d in op.deps:
                    if d is op:
                        continue
                    if d.eng == PE and op.eng == PE and not d.is_dma:
                        continue
                    d.signal = True
        for ds in self.all_dsems:
            ds.sem = newsem()
        for e in ENGS:
            n = 0
            sem = None
            for op in self.ops[e]:
                if op.is_dma:
                    ds, cnt = op.sig
                    op.sig = (ds.sem, cnt)
                elif op.signal:
                    if n % SEM_EPOCH == 0:
                        sem = newsem()
                    op.sig = (sem, n % SEM_EPOCH + 1)
                    n += 1
        block = es.enter_context(nc.Block())

        def run(eng_obj, e):
            waited = {}
            for op in self.ops[e]:
                need = {}
                for d in op.deps:
                    if d is op or d.sig is None:
                        continue
                    if d.eng == PE and e == PE and not d.is_dma:
                        continue
                    s, c = d.sig
                    k = id(s)
                    if waited.get(k, 0) >= c:
                        continue
                    if k not in need or need[k][1] < c:
                        need[k] = (s, c)
                for k, (s, c) in need.items():
                    eng_obj.wait_ge(s, c)
                    waited[k] = c
                if op.is_dma:
                    op.fn(eng_obj)
                else:
                    ins = op.fn(eng_obj)
                    if op.signal:
                        ins.then_inc(op.sig[0], 1)

        @block.tensor
        def _(pe):
            run(pe, PE)

        @block.scalar
        def _(a):
            run(a, ACT)

        @block.vector
        def _(v):
            run(v, DVE)

        @block.gpsimd
        def _(g):
            run(g, POOL)

        @block.sync
        def _(s):
            run(s, SP)


class PsumPool:
    def __init__(self, C, es, n, name, dtype=F32, cols=512):
        self.tiles = [es.enter_context(C.nc.psum_tensor("%s%d%s" % (name, i, C.sfx), [128, cols], dtype)) for i in range(n)]
        self.bufs = [C.P.buf(name) for _ in range(n)]
        self.i = 0
        self.n = n

    def next(self):
        i = self.i
        self.i = (i + 1) % self.n
        return self.tiles[i], self.bufs[i]


class Ctx:
    pass


def hs(h):
    return slice(h * 512, (h + 1) * 512)


def emit_norm(C, xsrc, Bx, ncols, gcol, sq, Bsq, rstd, Brstd, hT, BhT, pp, out_dtype_note=None):
    P = C.P
    P.add(ACT, lambda e: e.activation(out=sq, in_=xsrc, func=AF.Square), reads=[Bx], writes=[Bsq])
    ps, Bp = pp.next()
    for k in range(8):
        P.add(PE, lambda e, k=k: e.matmul(ps[:, :ncols], C.ones[:, :], sq[:, k, :], start=(k == 0), stop=(k == 7)),
              reads=[Bsq, C.Bconst], writes=[Bp])
    P.add(ACT, lambda e: e.activation(out=rstd, in_=ps[:, :ncols], func=AF.Sqrt, scale=1.0 / D, bias=C.epsn[:, 0:1]),
          reads=[Bp, C.Bconst], writes=[Brstd])
    P.add(DVE, lambda e: e.reciprocal(rstd, rstd), reads=[Brstd], writes=[Brstd])
    for k in range(8):
        P.add(DVE, lambda e, k=k: e.scalar_tensor_tensor(out=hT[:, k, :], in0=xsrc[:, k, :],
                                                         scalar=C.vecs[:, gcol + k:gcol + k + 1], in1=rstd,
                                                         op0=ALU.mult, op1=ALU.mult),
              reads=[Bx, Brstd, C.Bconst], writes=[BhT])


def wview(ap2d):
    return ap2d.rearrange("(c p) n -> p c n", p=128)


WNAMES = ("ffn1_w_in", "ffn1_w_out", "w_in", "w_branch_pool", "w_branch_attn", "w_branch_lru",
          "w_out", "ffn2_w_in", "ffn2_w_out")


def conv_pairs(C, name, l):
    src = C.inp[name][l]
    dst = C.wb[(name, l)]
    R, N = dst.shape
    pairs = []
    cstep = 2816 if N > 2816 else N
    for r in range(0, R, 128):
        for c in range(0, N, cstep):
            cc = min(cstep, N - c)
            pairs.append((dst[r:r + 128, c:c + cc], src[r:r + 128, c:c + cc]))
    return pairs


def setup_weights(C, lazy=True):
    P = C.P
    C.Bwb = {}
    C.convq = []
    for l in range(DEPTH):
        for name in WNAMES:
            C.Bwb[(name, l)] = Buf("wb_%s_%d" % (name, l))
    order = [(n, 0) for n in ("ffn1_w_in", "ffn1_w_out", "w_in", "w_branch_pool", "w_branch_attn", "w_branch_lru",
                              "w_out", "ffn2_w_in", "ffn2_w_out")] + [(n, 1) for n in WNAMES]
    for (name, l) in order:
        for pr in conv_pairs(C, name, l):
            C.convq.append(((name, l), pr))

    def pending(n):
        k = len(C.convq) if n is None else min(n, len(C.convq))
        while k > 0:
            key = C.convq[0][0]
            prs = []
            while k > 0 and C.convq and C.convq[0][0] == key:
                prs.append(C.convq.pop(0)[1])
                k -= 1
            B = C.Bwb[key]
            P.dma(POOL, prs, B, writes=[B])

    def need(keys):
        while any(k in keys for k, _ in C.convq):
            pending(1)
    C.pending_conversions = pending
    C.need_weights = need


def stage_ffn(C, l, which):
    nc, P = C.nc, C.P
    gcol = l * LV + (V_FFN1 if which == 1 else V_FFN2)
    nin, nout = "ffn%d_w_in" % which, "ffn%d_w_out" % which
    wi = wview(C.wb[(nin, l)])
    wo_d = wview(C.wb[(nout, l)])
    Bwi, Bwod = C.Bwb[(nin, l)], C.Bwb[(nout, l)]
    xv = wview(C.xT)
    NCH = T // 1024
    with ExitStack() as es:
        def sb(name, shape, dt):
            return es.enter_context(nc.sbuf_tensor(name + C.sfx, shape, dt))
        xc = [sb("f_xc%d" % i, [128, 8, 1024], F32) for i in range(2)]
        sq = sb("f_sq", [128, 8, 512], BF16)
        rstd = sb("f_rstd", [128, 1024], F32)
        hT = sb("f_hT", [128, 8, 1024], BF16)
        wg = [sb("f_wg%d" % i, [128, 8, 256], BF16) for i in range(2)]
        wu = [sb("f_wu%d" % i, [128, 8, 256], BF16) for i in range(2)]
        gT = sb("f_gT", [128, 22, 1024], BF16)
        wo = sb("f_wo", [128, 22, 1024], BF16)
        tmp = [sb("f_tmp%d" % i, [128, 512], F32) for i in range(2)]
        Bsq, Bwo = P.buf("sq"), P.buf("wo")
        Bxc = [P.buf("xc") for _ in range(2)]
        Brstd = [P.buf("rstd") for _ in range(2)]
        BhT = [P.buf("hT") for _ in range(2)]
        Bw = [P.buf("w") for _ in range(2)]
        Bg = [P.buf("g") for _ in range(2)]
        Btmp = [P.buf("tmp") for _ in range(2)]
        pp = PsumPool(C, es, 8, "f_ps")

        def load_x(c):
            t0 = c * 1024
            b = c % 2
            P.dma(SP, [(xc[b][:, 0:4, :], xv[:, 0:4, t0:t0 + 1024]), (xc[b][:, 4:8, :], xv[:, 4:8, t0:t0 + 1024])],
                  Bxc[b], writes=[Bxc[b]])

        def norm_stats(c):
            b = c % 2
            for h in range(2):
                xs = xc[b][:, :, hs(h)]
                P.add(ACT, lambda e, xs=xs: e.activation(out=sq[:], in_=xs, func=AF.Square), reads=[Bxc[b]], writes=[Bsq])
                ps, Bp = pp.next()
                for k in range(8):
                    P.add(PE, lambda e, k=k, ps=ps: e.matmul(ps[:, :], C.ones[:, :], sq[:, k, :], start=(k == 0), stop=(k == 7)),
                          reads=[Bsq, C.Bconst], writes=[Bp])
                P.add(ACT, lambda e, ps=ps, h=h: e.activation(out=rstd[:, hs(h)], in_=ps[:, :], func=AF.Sqrt, scale=1.0 / D,
                                                              bias=C.epsn[:, 0:1]), reads=[Bp, C.Bconst], writes=[Brstd[h]])
                P.add(DVE, lambda e, h=h: e.reciprocal(rstd[:, hs(h)], rstd[:, hs(h)]), reads=[Brstd[h]], writes=[Brstd[h]])

        def norm_apply(c):
            b = c % 2
            for h in range(2):
                for k in range(8):
                    P.add(DVE, lambda e, k=k, h=h, b=b: e.scalar_tensor_tensor(
                        out=hT[:, k, hs(h)], in0=xc[b][:, k, hs(h)], scalar=C.vecs[:, gcol + k:gcol + k + 1],
                        in1=rstd[:, hs(h)], op0=ALU.mult, op1=ALU.mult),
                        reads=[Bxc[b], Brstd[h], C.Bconst], writes=[BhT[h]])

        C.need_weights({(nin, l), (nout, l)})
        P.dma(POOL, [(wo[:, 0:11, :], wo_d[:, 0:11, :]), (wo[:, 11:22, :], wo_d[:, 11:22, :])], Bwo,
              reads=[Bwod], writes=[Bwo])
        load_x(0)
        norm_stats(0)
        norm_apply(0)
        wcnt = 0
        tcnt = 0
        for c in range(NCH):
            t0 = c * 1024
            xb_ = c % 2
            if c + 1 < NCH:
                load_x(c + 1)
            for fg in range(11):
                wbi = wcnt % 2
                wcnt += 1
                P.dma(POOL, [(wg[wbi][:], wi[:, :, fg * 256:fg * 256 + 256]),
                             (wu[wbi][:], wi[:, :, FF + fg * 256:FF + fg * 256 + 256])],
                      Bw[wbi], reads=[Bwi], writes=[Bw[wbi]])
                if fg % 2 == 0:
                    C.pending_conversions(1)
                for j in range(2):
                    f = fg * 2 + j
                    for h in range(2):
                        psg, Bpg = pp.next()
                        psu, Bpu = pp.next()
                        for k in range(8):
                            P.add(PE, lambda e, k=k, j=j, h=h, psg=psg, wbi=wbi: e.matmul(
                                psg[:, :], wg[wbi][:, k, j * 128:(j + 1) * 128], hT[:, k, hs(h)],
                                start=(k == 0), stop=(k == 7)), reads=[Bw[wbi], BhT[h]], writes=[Bpg])
                        for k in range(8):
                            P.add(PE, lambda e, k=k, j=j, h=h, psu=psu, wbi=wbi: e.matmul(
                                psu[:, :], wu[wbi][:, k, j * 128:(j + 1) * 128], hT[:, k, hs(h)],
                                start=(k == 0), stop=(k == 7)), reads=[Bw[wbi], BhT[h]], writes=[Bpu])
                        ti = tcnt % 2
                        tcnt += 1
                        P.add(ACT, lambda e, psg=psg, ti=ti: e.activation(out=tmp[ti][:], in_=psg[:, :], func=AF.Silu),
                              reads=[Bpg], writes=[Btmp[ti]])
                        P.add(DVE, lambda e, psu=psu, ti=ti, f=f, h=h: e.tensor_tensor(
                            out=gT[:, f, hs(h)], in0=tmp[ti][:], in1=psu[:, :], op=ALU.mult),
                            reads=[Btmp[ti], Bpu], writes=[Bg[h]])
            if c + 1 < NCH:
                norm_stats(c + 1)
                norm_apply(c + 1)
            for d in range(8):
                for h in range(2):
                    ps, Bp = pp.next()
                    for fc in range(22):
                        P.add(PE, lambda e, fc=fc, d=d, h=h, ps=ps: e.matmul(
                            ps[:, :], wo[:, fc, d * 128:(d + 1) * 128], gT[:, fc, hs(h)],
                            start=(fc == 0), stop=(fc == 21)), reads=[Bwo, Bg[h]], writes=[Bp])
                    P.add(DVE, lambda e, d=d, h=h, ps=ps, xb_=xb_: e.scalar_tensor_tensor(
                        out=xc[xb_][:, d, hs(h)], in0=ps[:, :], scalar=0.5, in1=xc[xb_][:, d, hs(h)],
                        op0=ALU.mult, op1=ALU.add), reads=[Bp, Bxc[xb_]], writes=[Bxc[xb_]])
            P.dma(SP, [(xv[:, 0:4, t0:t0 + 1024], xc[xb_][:, 0:4, :]), (xv[:, 4:8, t0:t0 + 1024], xc[xb_][:, 4:8, :])],
                  Bxc[xb_], reads=[Bxc[xb_]])
        P.barrier()


def stage_inproj(C, l):
    nc, P = C.nc, C.P
    gcol = l * LV + V_MIX
    wi = wview(C.wb[("w_in", l)])
    xv = wview(C.xT)
    htv = wview(C.HT)
    with ExitStack() as es:
        def sb(name, shape, dt):
            return es.enter_context(nc.sbuf_tensor(name + C.sfx, shape, dt))
        w1 = sb("i_w1", [128, 8, 2304], BF16)
        xb = [sb("i_xb%d" % i, [128, 8, 512], F32) for i in range(2)]
        sq = sb("i_sq", [128, 8, 512], BF16)
        rstd = sb("i_rstd", [128, 512], F32)
        hT = [sb("i_hT%d" % i, [128, 8, 512], BF16) for i in range(2)]
        groups = [
            ("p", 0, 2, C.PT, F32, None),
            ("q", 2, 4, C.QT, BF16, 0.125),
            ("k", 6, 4, C.KT, BF16, None),
            ("lx", 14, 2, C.LXT, F32, None),
            ("lg", 16, 2, C.LGT, F32, None),
        ]
        stg = {g[0]: [sb("i_s%s%d" % (g[0], i), [128, g[2], 512], g[4]) for i in range(2)] for g in groups}
        stgv = [sb("i_sv%d" % i, [128, 4, 512], BF16) for i in range(2)]
        Bw1, Bsq, Brstd = P.buf(), P.buf(), P.buf()
        Bxb = [P.buf() for _ in range(2)]
        BhT = [P.buf() for _ in range(2)]
        Bstg = {g[0]: [P.buf() for _ in range(2)] for g in groups}
        Bstgv = [P.buf() for _ in range(2)]
        pp = PsumPool(C, es, 8, "i_ps")
        C.need_weights({("w_in", l)})
        P.dma(POOL, [(w1[:, 0:4, :], wi[:, 0:4, 0:2304]), (w1[:, 4:8, :], wi[:, 4:8, 0:2304])], Bw1,
              reads=[C.Bwb[("w_in", l)]], writes=[Bw1])
        ev = 0
        for n in range(T // 512):
            t0 = n * 512
            pb = n % 2
            P.dma(SP, [(xb[pb][:], xv[:, :, t0:t0 + 512])], Bxb[pb], writes=[Bxb[pb]])
            emit_norm(C, xb[pb][:], Bxb[pb], 512, gcol, sq[:], Bsq, rstd[:], Brstd, hT[pb][:], BhT[pb], pp)
            P.dma(SP, [(htv[:, :, t0:t0 + 512], hT[pb][:])], BhT[pb], reads=[BhT[pb]])
            for (name, t1, ntile, dten, dt, scale) in groups:
                st = stg[name][pb]
                Bs = Bstg[name][pb]
                for j in range(ntile):
                    col = (t1 + j) * 128
                    ps, Bp = pp.next()
                    for k in range(8):
                        P.add(PE, lambda e, k=k, col=col, ps=ps, pb=pb: e.matmul(
                            ps[:, :], w1[:, k, col:col + 128], hT[pb][:, k, :], start=(k == 0), stop=(k == 7)),
                            reads=[Bw1, BhT[pb]], writes=[Bp])
                    if scale is not None:
                        P.add(ACT, lambda e, ps=ps, st=st, j=j, scale=scale: e.mul(st[:, j, :], ps[:, :], scale),
                              reads=[Bp], writes=[Bs])
                    elif ev % 2 == 0:
                        P.add(ACT, lambda e, ps=ps, st=st, j=j: e.copy(out=st[:, j, :], in_=ps[:, :]),
                              reads=[Bp], writes=[Bs])
                    else:
                        P.add(DVE, lambda e, ps=ps, st=st, j=j: e.tensor_copy(st[:, j, :], ps[:, :]),
                              reads=[Bp], writes=[Bs])
                    ev += 1
                dv = dten.rearrange("(j p) t -> p j t", p=128)
                P.dma(SP, [(dv[:, :, t0:t0 + 512], st[:])], Bs, reads=[Bs])
            for tt in range(4):
                ps, Bp = pp.next()
                for k in range(8):
                    P.add(PE, lambda e, k=k, tt=tt, ps=ps, pb=pb: e.matmul(
                        ps[:, :], hT[pb][:, k, tt * 128:(tt + 1) * 128], w1[:, k, 1280:1792],
                        start=(k == 0), stop=(k == 7)), reads=[Bw1, BhT[pb]], writes=[Bp])
                P.add(DVE, lambda e, ps=ps, tt=tt, pb=pb: e.tensor_copy(stgv[pb][:, tt, :], ps[:, :]),
                      reads=[Bp], writes=[Bstgv[pb]])
            vv = C.V.rearrange("(n p) e -> p n e", p=128)
            P.dma(SP, [(vv[:, n * 4:n * 4 + 4, :], stgv[pb][:])], Bstgv[pb], reads=[Bstgv[pb]])
        P.barrier()


def stage_pool(C, l):
    nc, P = C.nc, C.P
    PADL = 16
    WID = S + 32
    with ExitStack() as es:
        def sb(name, shape, dt):
            return es.enter_context(nc.sbuf_tensor(name + C.sfx, shape, dt))
        pp_ = sb("p_pp", [128, WID], F32)
        A = [sb("p_a%d" % i, [128, WID], F32) for i in range(2)]
        mix = sb("p_mix", [128, S], F32)
        mixb = sb("p_mixb", [128, S], BF16)
        invc = sb("p_invc", [128, S], F32)
        wbd = sb("p_wbd", [128, 128], BF16)
        wst = sb("p_wst", [128, 128], F32)
        outb = [sb("p_out%d" % i, [128, S], BF16) for i in range(2)]
        Bpp, Bmix, Bmixb, Binvc, Bwbd, Bwst = P.buf(), P.buf(), P.buf(), P.buf(), P.buf(), P.buf()
        BA = [P.buf() for _ in range(2)]
        Bout = [P.buf() for _ in range(2)]
        pp = PsumPool(C, es, 4, "p_ps")
        P.add(DVE, lambda e: e.memset(pp_[:], 0.0), writes=[Bpp])
        it = 0
        for j in range(2):
            P.dma(SP, [(invc[:], C.inp["invc"][j])], Binvc, writes=[Binvc])
            P.add(DVE, lambda e: e.memset(wst[:], 0.0), writes=[Bwst])
            P.dma(SP, [(wst[0:64, 0:64], C.inp["pool_w"][l, 2 * j]), (wst[64:128, 64:128], C.inp["pool_w"][l, 2 * j + 1])],
                  Bwst, writes=[Bwst])
            P.add(DVE, lambda e: e.tensor_copy(wbd[:], wst[:]), reads=[Bwst], writes=[Bwbd])
            for b in range(2):
                P.dma(SP, [(pp_[:, PADL:PADL + S], C.PT[j * 128:(j + 1) * 128, b * S:(b + 1) * S])], Bpp, writes=[Bpp])
                wins = POOL_WINDOWS[2 * j:2 * j + 2]
                src = pp_
                Bsrc = Bpp
                w = 1
                k = 0
                have = {}
                while w < max(wins):
                    dst = A[k % 2]
                    Bd = BA[k % 2]
                    P.add(DVE, lambda e, src=src, dst=dst, w=w: e.tensor_tensor(
                        out=dst[:, w:WID], in0=src[:, w:WID], in1=src[:, 0:WID - w], op=ALU.add),
                        reads=[Bsrc], writes=[Bd])
                    w *= 2
                    src, Bsrc = dst, Bd
                    k += 1
                    for hh, ww in enumerate(wins):
                        if ww == w:
                            o = PADL + w // 2 - 1
                            ps_ = slice(hh * 64, hh * 64 + 64)
                            P.add(DVE, lambda e, src=src, o=o, ps_=ps_: e.tensor_tensor(
                                out=mix[ps_, :], in0=src[ps_, o:o + S], in1=invc[ps_, :], op=ALU.mult),
                                reads=[Bsrc, Binvc], writes=[Bmix])
                P.add(DVE, lambda e: e.tensor_tensor(out=mixb[:], in0=mix[:], in1=pp_[:, PADL:PADL + S], op=ALU.subtract),
                      reads=[Bmix, Bpp], writes=[Bmixb])
                ob = outb[it % 2]
                Bo = Bout[it % 2]
                it += 1
                for n in range(S // 512):
                    ps, Bp = pp.next()
                    P.add(PE, lambda e, ps=ps, n=n: e.matmul(ps[:, :], wbd[:, :], mixb[:, n * 512:(n + 1) * 512],
                                                             start=True, stop=True), reads=[Bwbd, Bmixb], writes=[Bp])
                    vc = l * LV + V_PSCALE + j
                    P.add(ACT, lambda e, ps=ps, n=n, ob=ob, vc=vc: e.activation(
                        out=ob[:, n * 512:(n + 1) * 512], in_=ps[:, :], func=AF.Copy, scale=C.vecs[:, vc:vc + 1]),
                        reads=[Bp, C.Bconst], writes=[Bo])
                P.dma(SP, [(C.YP[j * 128:(j + 1) * 128, b * S:(b + 1) * S], ob[:])], Bo, reads=[Bo])
        P.barrier()


def stage_lru(C, l):
    nc, P = C.nc, C.P
    with ExitStack() as es:
        def sb(name, shape, dt):
            return es.enter_context(nc.sbuf_tensor(name + C.sfx, shape, dt))
        lx = sb("r_lx", [128, S + 4], F32)
        lg = sb("r_lg", [128, S], F32)
        xf = sb("r_xf", [128, S], F32)
        xfb = sb("r_xfb", [128, S], BF16)
        ra = sb("r_ra", [128, S], F32)
        iu = sb("r_iu", [128, S], F32)
        tm = sb("r_tm", [128, S], F32)
        hh = [sb("r_h%d" % i, [128, S], F32) for i in range(2)]
        yb = sb("r_yb", [128, S], BF16)
        wst = sb("r_wst", [128, 128], F32)
        wa = [sb("r_wa%d" % i, [128, 128], BF16) for i in range(2)]
        wx = [sb("r_wx%d" % i, [128, 128], BF16) for i in range(2)]
        sc = sb("r_sc", [128, 8], F32)
        Blx, Blg, Bxf, Bxfb, Bra, Biu, Btm, Byb, Bwst, Bsc = (P.buf() for _ in range(10))
        Bh = [P.buf() for _ in range(2)]
        Bwa = [P.buf() for _ in range(2)]
        Bwx = [P.buf() for _ in range(2)]
        pp = PsumPool(C, es, 8, "r_ps")
        P.add(DVE, lambda e: e.memset(lx[:], 0.0), writes=[Blx])
        vb = l * LV
        for j in range(2):
            for dr in range(2):
                lc = vb + V_LAM + dr * 2 + j
                P.add(ACT, lambda e, lc=lc, dr=dr: e.activation(
                    out=sc[:, 4 + dr:5 + dr], in_=C.vecs[:, lc:lc + 1], func=AF.Abs),
                    reads=[C.Bconst], writes=[Bsc])
                P.add(ACT, lambda e, dr=dr: e.activation(out=sc[:, 4 + dr:5 + dr], in_=sc[:, 4 + dr:5 + dr],
                                                         func=AF.Exp, scale=-1.0), reads=[Bsc], writes=[Bsc])
                P.add(ACT, lambda e, dr=dr: e.activation(out=sc[:, 4 + dr:5 + dr], in_=sc[:, 4 + dr:5 + dr],
                                                         func=AF.Ln, bias=C.epsn[:, 1:2]), reads=[Bsc, C.Bconst], writes=[Bsc])
                P.add(DVE, lambda e, lc=lc, dr=dr: e.tensor_scalar(
                    out=sc[:, 6 + dr:7 + dr], in0=C.vecs[:, lc:lc + 1], scalar1=-1.0, scalar2=0.0,
                    op0=ALU.mult, op1=ALU.max), reads=[C.Bconst, Bsc], writes=[Bsc])
                P.add(DVE, lambda e, dr=dr: e.tensor_tensor(out=sc[:, dr:dr + 1], in0=sc[:, 4 + dr:5 + dr],
                                                            in1=sc[:, 6 + dr:7 + dr], op=ALU.add), reads=[Bsc], writes=[Bsc])
                P.add(DVE, lambda e, dr=dr: e.tensor_scalar(out=sc[:, dr:dr + 1], in0=sc[:, dr:dr + 1], scalar1=-8.0,
                                                            scalar2=None, op0=ALU.mult), reads=[Bsc], writes=[Bsc])
                for (wsrc, wdst, Bwd) in ((C.inp["lru_w_a"], wa[dr], Bwa[dr]), (C.inp["lru_w_x"], wx[dr], Bwx[dr])):
                    P.add(DVE, lambda e: e.memset(wst[:], 0.0), writes=[Bwst])
                    P.dma(SP, [(wst[0:64, 0:64], wsrc[l, dr, 2 * j]), (wst[64:128, 64:128], wsrc[l, dr, 2 * j + 1])],
                          Bwst, writes=[Bwst])
                    P.add(DVE, lambda e, wdst=wdst: e.tensor_copy(wdst[:], wst[:]), reads=[Bwst], writes=[Bwd])
            for b in range(2):
                rows = slice(j * 128, (j + 1) * 128)
                cols = slice(b * S, (b + 1) * S)
                P.dma(SP, [(lx[:, 2:2 + S], C.LXT[rows, cols])], Blx, writes=[Blx])
                P.dma(SP, [(lg[:], C.LGT[rows, cols])], Blg, writes=[Blg])
                cw = vb + V_CONVW
                cbc = vb + V_CONVB + j
                P.add(DVE, lambda e, cw=cw, cbc=cbc, j=j: e.tensor_scalar(
                    out=xf[:], in0=lx[:, 0:S], scalar1=C.vecs[:, cw + j:cw + j + 1], scalar2=C.vecs[:, cbc:cbc + 1],
                    op0=ALU.mult, op1=ALU.add), reads=[Blx, C.Bconst], writes=[Bxf])
                for i in range(1, 4):
                    P.add(DVE, lambda e, i=i, cw=cw, j=j: e.scalar_tensor_tensor(
                        out=xf[:], in0=lx[:, i:i + S], scalar=C.vecs[:, cw + 2 * i + j:cw + 2 * i + j + 1], in1=xf[:],
                        op0=ALU.mult, op1=ALU.add), reads=[Blx, Bxf, C.Bconst], writes=[Bxf])
                P.add(ACT, lambda e: e.copy(out=xfb[:], in_=xf[:]), reads=[Bxf], writes=[Bxfb])
                for dr in range(2):
                    bac = vb + V_BA + dr * 2 + j
                    bxc = vb + V_BX + dr * 2 + j
                    for (wt, Bwt, dst, Bdst, bc) in ((wa[dr], Bwa[dr], ra, Bra, bac), (wx[dr], Bwx[dr], iu, Biu, bxc)):
                        for n in range(S // 512):
                            ps, Bp = pp.next()
                            P.add(PE, lambda e, ps=ps, n=n, wt=wt: e.matmul(
                                ps[:, :], wt[:, :], xfb[:, n * 512:(n + 1) * 512], start=True, stop=True),
                                reads=[Bwt, Bxfb], writes=[Bp])
                            P.add(ACT, lambda e, ps=ps, n=n, dst=dst, bc=bc: e.activation(
                                out=dst[:, n * 512:(n + 1) * 512], in_=ps[:, :], func=AF.Sigmoid,
                                bias=C.vecs[:, bc:bc + 1]), reads=[Bp, C.Bconst], writes=[Bdst])
                    P.add(ACT, lambda e, dr=dr: e.activation(out=ra[:], in_=ra[:], func=AF.Exp, scale=sc[:, dr:dr + 1]),
                          reads=[Bra, Bsc], writes=[Bra])
                    P.add(DVE, lambda e: e.tensor_tensor(out=tm[:], in0=ra[:], in1=ra[:], op=ALU.mult),
                          reads=[Bra], writes=[Btm])
                    P.add(ACT, lambda e: e.activation(out=tm[:], in_=tm[:], func=AF.Sqrt, scale=-1.0, bias=C.epsn[:, 1:2]),
                          reads=[Btm, C.Bconst], writes=[Btm])
                    P.add(DVE, lambda e: e.tensor_tensor(out=iu[:], in0=iu[:], in1=xf[:], op=ALU.mult),
                          reads=[Biu, Bxf], writes=[Biu])
                    P.add(DVE, lambda e: e.tensor_tensor(out=iu[:], in0=iu[:], in1=tm[:], op=ALU.mult),
                          reads=[Biu, Btm], writes=[Biu])
                    if dr == 0:
                        P.add(DVE, lambda e: e.tensor_tensor_scan(hh[0][:], ra[:], iu[:], 0.0, ALU.mult, ALU.add),
                              reads=[Bra, Biu], writes=[Bh[0]])
                    else:
                        def rev(t):
                            a = t[:]
                            pstep = a.ap[0][0]
                            return bass.AP(a.tensor, a.offset + (S - 1), [[pstep, 128], [-1, S]])
                        P.add(DVE, lambda e, rev=rev: e.tensor_tensor_scan(rev(hh[1]), rev(ra), rev(iu), 0.0, ALU.mult, ALU.add),
                              reads=[Bra, Biu], writes=[Bh[1]])
                P.add(DVE, lambda e: e.tensor_tensor(out=hh[0][:], in0=hh[0][:], in1=hh[1][:], op=ALU.add),
                      reads=[Bh[0], Bh[1]], writes=[Bh[0]])
                P.add(ACT, lambda e: e.activation(out=tm[:], in_=lg[:], func=AF.Gelu), reads=[Blg, Btm], writes=[Btm])
                P.add(DVE, lambda e: e.tensor_tensor(out=yb[:], in0=tm[:], in1=hh[0][:], op=ALU.mult),
                      reads=[Btm, Bh[0]], writes=[Byb])
                P.dma(SP, [(C.YL[rows, cols], yb[:])], Byb, reads=[Byb])
        P.barrier()


def stage_attn(C, l):
    nc, P = C.nc, C.P
    lam_init = 0.8 - 0.6 * math.exp(-0.3 * l)
    cfac = 1.0 - lam_init
    OOFF = (0, 160, 320)
    with ExitStack() as es:
        def sb(name, shape, dt):
            return es.enter_context(nc.sbuf_tensor(name + C.sfx, shape, dt))
        NSET = 2
        KA = [[sb("a_ka%d_%d" % (s_, m), [69, S], BF16) for m in range(2)] for s_ in range(NSET)]
        QA = [[[sb("a_qa%d_%d_%d" % (s_, m, sg), [69, S], BF16) for sg in range(2)] for m in range(2)] for s_ in range(NSET)]
        VP = [sb("a_vp%d" % s_, [128, 32, 129], BF16) for s_ in range(NSET)]
        BKA = [P.buf() for _ in range(NSET)]
        BQA = [P.buf() for _ in range(NSET)]
        BVP = [P.buf() for _ in range(NSET)]
        NPT = 3
        PT = [sb("a_pt%d" % i, [128, 1024], BF16) for i in range(NPT)]
        BPT = [P.buf() for _ in range(NPT)]
        lamt = sb("a_lam", [128, 256], F32)
        lams = sb("a_lams", [128, 8], F32)
        sg_t = sb("a_sg", [128, 128], F32)
        zz = sb("a_zz", [128, 16], F32)
        o_t = [sb("a_o%d" % i, [128, 128], F32) for i in range(4)]
        junk = sb("a_junk", [128, 128], F32)
        ob = [sb("a_ob%d" % i, [128, 128], BF16) for i in range(4)]
        ost = [sb("a_ost%d" % i, [128, 512], BF16) for i in range(2)]
        Blam, Bsg, Bzz, Bjunk = P.buf(), P.buf(), P.buf(), P.buf()
        Bo = [P.buf() for _ in range(4)]
        Bob = [P.buf() for _ in range(4)]
        Bost = [P.buf() for _ in range(2)]
        SP_ = [es.enter_context(nc.psum_tensor("a_ps%d%s" % (i, C.sfx), [128, 1024], F32)) for i in range(2)]
        BS = [P.buf() for _ in range(2)]
        OP_ = [es.enter_context(nc.psum_tensor("a_po%d%s" % (i, C.sfx), [128, 512], F32)) for i in range(3)]
        BO = [P.buf() for _ in range(3)]
        TP_ = es.enter_context(nc.psum_tensor("a_ptr" + C.sfx, [128, 1024], BF16))
        BT = P.buf()

        lsrc = C.inp["attn_lambda"][l]
        lap = bass.AP(lsrc.tensor, lsrc.offset, [[0, 128], [1, 256]])
        P.dma(SP, [(lamt[:], lap)], Blam, writes=[Blam])
        P.add(DVE, lambda e: e.memset(lams[:], 0.0), writes=[Blam])
        for i in range(2):
            P.add(DVE, lambda e, i=i: e.tensor_tensor(out=lamt[:, i * 128:i * 128 + 64], in0=lamt[:, i * 128:i * 128 + 64],
                                                      in1=lamt[:, i * 128 + 64:i * 128 + 128], op=ALU.mult),
                  reads=[Blam], writes=[Blam])
            P.add(DVE, lambda e, i=i: e.reduce_sum(out=lams[:, i:i + 1], in_=lamt[:, i * 128:i * 128 + 64],
                                                   axis=mybir.AxisListType.X), reads=[Blam], writes=[Blam])
        P.add(ACT, lambda e: e.activation(out=lams[:, 0:2], in_=lams[:, 0:2], func=AF.Exp), reads=[Blam], writes=[Blam])
        P.add(DVE, lambda e: e.tensor_tensor(out=lams[:, 2:3], in0=lams[:, 0:1], in1=lams[:, 1:2], op=ALU.subtract),
              reads=[Blam], writes=[Blam])
        P.add(DVE, lambda e: e.tensor_scalar(out=lams[:, 3:4], in0=lams[:, 2:3], scalar1=lam_init, scalar2=-1.0,
                                             op0=ALU.add, op1=ALU.mult), reads=[Blam], writes=[Blam])
        ssrc = C.inp["attn_subln"][l]
        sap = bass.AP(ssrc.tensor, ssrc.offset, [[0, 128], [1, 128]])
        P.dma(SP, [(sg_t[:], sap)], Bsg, writes=[Bsg])
        for s_ in range(NSET):
            P.add(DVE, lambda e, s_=s_: e.memset(VP[s_][:, :, 128:129], 1.0), writes=[BVP[s_]])
            P.dma(SP, [(KA[s_][m][64:69, :], C.inp["kaug"][:, :]) for m in range(2)], BKA[s_], writes=[BKA[s_]])

        def load_set(b, h, s_):
            cols = slice(b * S, (b + 1) * S)
            prs = []
            for m in range(2):
                r0 = h * 128 + m * 64
                prs.append((KA[s_][m][0:64, :], C.KT[r0:r0 + 64, cols]))
            P.dma(SP, prs, BKA[s_], writes=[BKA[s_]])
            prs = []
            for m in range(2):
                r0 = h * 128 + m * 64
                for sg in range(2):
                    prs.append((QA[s_][m][sg][0:64, :], C.QT[r0:r0 + 64, cols]))
                    prs.append((QA[s_][m][sg][64:69, :], C.inp["qaug"][h, sg]))
            P.dma(SP, prs, BQA[s_], writes=[BQA[s_]])
            vv = C.V.rearrange("(n p) e -> p n e", p=128)
            P.dma(SP, [(VP[s_][:, 8 * i:8 * i + 8, 0:128], vv[:, b * 32 + 8 * i:b * 32 + 8 * i + 8, h * 128:(h + 1) * 128])
                       for i in range(4)], BVP[s_], writes=[BVP[s_]])

        C.pending_conversions(None)
        bh_list = [(b, h) for b in range(2) for h in range(4)][:DBG.get('bh', 8)]
        load_set(0, 0, 0)
        NQB = DBG.get('qb', 8)
        descs = []
        for bi, (b, h) in enumerate(bh_list):
            for qb in range(NQB):
                for kt in range(32):
                    descs.append((bi, b, h, bi % NSET, qb, kt, len(descs)))

        def emit_qk(dsc):
            bi, b, h, s_, qb, kt, it = dsc
            q0 = qb * 512
            k0 = kt * 128
            si = it % 2
            Sp = SP_[si]
            jd = kt - 4 * qb
            for m in range(2):
                base = m * 512
                ka = KA[s_][m][0:69, k0:k0 + 128]
                if jd < 0 or jd > 3:
                    sg = 0 if jd < 0 else 1
                    P.add(PE, lambda e, Sp=Sp, ka=ka, m=m, sg=sg, base=base, s_=s_, q0=q0: e.matmul(
                        Sp[:, base:base + 512], ka, QA[s_][m][sg][0:69, q0:q0 + 512], start=True, stop=True,
                        skip_group_check=True), reads=[BKA[s_], BQA[s_]], writes=[BS[si]])
                else:
                    first = True
                    if jd > 0:
                        P.add(PE, lambda e, Sp=Sp, ka=ka, m=m, base=base, s_=s_, q0=q0, jd=jd: e.matmul(
                            Sp[:, base:base + 128 * jd], ka, QA[s_][m][1][0:69, q0:q0 + 128 * jd],
                            start=True, stop=True, skip_group_check=True),
                            reads=[BKA[s_], BQA[s_]], writes=[BS[si]])
                        first = False
                    P.add(PE, lambda e, Sp=Sp, ka=ka, m=m, base=base, s_=s_, q0=q0, jd=jd, first=first: e.matmul(
                        Sp[:, base + 128 * jd:base + 512], ka, QA[s_][m][0][0:69, q0 + 128 * jd:q0 + 512],
                        start=first, stop=False, skip_group_check=True),
                        reads=[BKA[s_], BQA[s_]], writes=[BS[si]])
                    P.add(PE, lambda e, Sp=Sp, base=base, jd=jd, h=h: e.matmul(
                        Sp[:, base + 128 * jd:base + 128 * jd + 128], C.idS[:, h, :], C.c0[:, :],
                        start=False, stop=True, skip_group_check=True),
                        reads=[C.Bconst], writes=[BS[si]])

        def emit_exp_pv(dsc):
            bi, b, h, s_, qb, kt, it = dsc
            si = it % 2
            pi = it % NPT
            Sp = SP_[si]
            P.add(ACT, lambda e, Sp=Sp, pi=pi: e.activation(out=PT[pi][:], in_=Sp[:, :], func=AF.Exp),
                  reads=[BS[si]], writes=[BPT[pi]])
            for m in range(2):
                for qt in range(4):
                    a = m * 4 + qt
                    bank, slot = a // 3, a % 3
                    oo = OOFF[slot]
                    P.add(PE, lambda e, pi=pi, m=m, qt=qt, bank=bank, oo=oo, slot=slot, kt=kt, s_=s_: e.matmul(
                        OP_[bank][:, oo:oo + 129], PT[pi][:, m * 512 + qt * 128:m * 512 + qt * 128 + 128],
                        VP[s_][:, kt, :], start=(kt == 0 and slot == 0), stop=(kt == 31),
                        skip_group_check=True), reads=[BPT[pi], BVP[s_]], writes=[BO[bank]])

        oc = 0
        emit_qk(descs[0])
        for n_, dsc in enumerate(descs):
            bi, b, h, s_, qb, kt, it = dsc
            q0 = qb * 512
            if qb == 0 and kt == 0 and bi + 1 < len(bh_list):
                load_set(bh_list[bi + 1][0], bh_list[bi + 1][1], (bi + 1) % NSET)
            if n_ + 1 < len(descs):
                emit_qk(descs[n_ + 1])
            emit_exp_pv(dsc)
            if kt == 31:
                for bank in range(3):
                    n = 3 if bank < 2 else 2
                    zsrc = bass.AP(OP_[bank][:].tensor, OP_[bank][:, 128:129].offset, [[OP_[bank][:].ap[0][0], 128], [160, n]])
                    P.add(DVE, lambda e, zsrc=zsrc, bank=bank, n=n: e.reciprocal(zz[:, bank * 3:bank * 3 + n], zsrc),
                          reads=[BO[bank]], writes=[Bzz])
                P.add(DVE, lambda e: e.tensor_scalar(out=zz[:, 4:8], in0=zz[:, 4:8], scalar1=lams[:, 3:4], scalar2=None,
                                                     op0=ALU.mult), reads=[Bzz, Blam], writes=[Bzz])
                osi = (oc // 4) % 2
                oc += 4
                for qt in range(4):
                    a1, a2 = qt, 4 + qt
                    b1, s1 = a1 // 3, a1 % 3
                    b2, s2 = a2 // 3, a2 % 3
                    P.add(DVE, lambda e, qt=qt, b1=b1, s1=s1, a1=a1: e.tensor_scalar(
                        out=o_t[qt][:], in0=OP_[b1][:, OOFF[s1]:OOFF[s1] + 128], scalar1=zz[:, a1:a1 + 1], scalar2=None,
                        op0=ALU.mult), reads=[BO[b1], Bzz], writes=[Bo[qt]])
                    P.add(DVE, lambda e, qt=qt, b2=b2, s2=s2, a2=a2: e.scalar_tensor_tensor(
                        out=o_t[qt][:], in0=OP_[b2][:, OOFF[s2]:OOFF[s2] + 128], scalar=zz[:, a2:a2 + 1], in1=o_t[qt][:],
                        op0=ALU.mult, op1=ALU.add), reads=[BO[b2], Bzz, Bo[qt]], writes=[Bo[qt]])
                for qt in range(4):
                    P.add(DVE, lambda e, qt=qt: e.tensor_tensor(out=junk[:], in0=o_t[qt][:], in1=o_t[qt][:], op=ALU.mult),
                          reads=[Bo[qt]], writes=[Bjunk])
                    P.add(DVE, lambda e, qt=qt: e.reduce_sum(out=zz[:, 8 + qt:9 + qt], in_=junk[:], axis=mybir.AxisListType.X),
                          reads=[Bjunk], writes=[Bzz])
                P.add(ACT, lambda e: e.activation(out=zz[:, 12:16], in_=zz[:, 8:12], func=AF.Ln,
                                                  scale=1.0 / (128.0 * cfac * cfac), bias=C.epsn[:, 2 + l:3 + l]),
                      reads=[Bzz, C.Bconst], writes=[Bzz])
                P.add(ACT, lambda e: e.activation(out=zz[:, 12:16], in_=zz[:, 12:16], func=AF.Exp, scale=-0.5),
                      reads=[Bzz], writes=[Bzz])
                for qt in range(4):
                    P.add(DVE, lambda e, qt=qt: e.scalar_tensor_tensor(
                        out=ob[qt][:], in0=o_t[qt][:], scalar=zz[:, 12 + qt:13 + qt], in1=sg_t[:], op0=ALU.mult, op1=ALU.mult),
                        reads=[Bo[qt], Bzz, Bsg], writes=[Bob[qt]])
                    P.add(PE, lambda e, qt=qt: e.transpose(TP_[:, qt * 128:(qt + 1) * 128], ob[qt][:], C.ident[:, :]),
                          reads=[Bob[qt], C.Bconst], writes=[BT])
                P.add(DVE, lambda e, osi=osi: e.tensor_copy(ost[osi][:], TP_[:, 0:512]), reads=[BT], writes=[Bost[osi]])
                P.dma(SP, [(C.YA[h * 128:(h + 1) * 128, b * S + q0:b * S + q0 + 512], ost[osi][:])], Bost[osi], reads=[Bost[osi]])
        P.barrier()


def stage_merge(C, l):
    nc, P = C.nc, C.P
    wi = wview(C.wb[("w_in", l)])
    xv = wview(C.xT)
    htv = wview(C.HT)
    with ExitStack() as es:
        def sb(name, shape, dt):
            return es.enter_context(nc.sbuf_tensor(name + C.sfx, shape, dt))
        wgt = sb("m_wg", [128, 8, 3072], BF16)
        wbr = [sb("m_wbp", [128, 2, 1024], BF16), sb("m_wba", [128, 4, 1024], BF16), sb("m_wbl", [128, 2, 1024], BF16)]
        wot = sb("m_wo", [128, 8, 1024], BF16)
        nbr = (2, 4, 2)
        ysrc = (C.YP, C.YA, C.YL)
        hT = [sb("m_hT%d" % i, [128, 8, 512], BF16) for i in range(2)]
        yin = [[sb("m_y%d_%d" % (j, i), [128, nbr[j], 512], BF16) for i in range(2)] for j in range(3)]
        xb = [sb("m_xb%d" % i, [128, 8, 512], F32) for i in range(2)]
        mT = sb("m_mT", [128, 8, 512], BF16)
        gt = [sb("m_g%d" % j, [128, 512], F32) for j in range(3)]
        tt = [sb("m_t%d" % j, [128, 512], F32) for j in range(3)]
        Bw = P.buf()
        BhT = [P.buf() for _ in range(2)]
        Byin = [P.buf() for _ in range(2)]
        Bxb = [P.buf() for _ in range(2)]
        BmT = P.buf()
        Bg = [P.buf() for _ in range(3)]
        Bt = [P.buf() for _ in range(3)]
        pp = PsumPool(C, es, 8, "m_ps")
        mw = ("w_in", "w_branch_pool", "w_branch_attn", "w_branch_lru", "w_out")
        C.need_weights({(n_, l) for n_ in mw})
        P.dma(POOL, [(wgt[:, 0:4, :], wi[:, 0:4, 2304:5376]), (wgt[:, 4:8, :], wi[:, 4:8, 2304:5376]),
                     (wbr[0][:], wview(C.wb[("w_branch_pool", l)])), (wbr[1][:], wview(C.wb[("w_branch_attn", l)])),
                     (wbr[2][:], wview(C.wb[("w_branch_lru", l)])), (wot[:], wview(C.wb[("w_out", l)]))], Bw,
              reads=[C.Bwb[(n_, l)] for n_ in mw], writes=[Bw])
        for n in range(T // 512):
            t0 = n * 512
            pb = n % 2
            P.dma(SP, [(hT[pb][:], htv[:, :, t0:t0 + 512])], BhT[pb], writes=[BhT[pb]])
            P.dma(SP, [(yin[j][pb][:], ysrc[j].rearrange("(c p) t -> p c t", p=128)[:, :, t0:t0 + 512]) for j in range(3)],
                  Byin[pb], writes=[Byin[pb]])
            P.dma(SP, [(xb[pb][:], xv[:, :, t0:t0 + 512])], Bxb[pb], writes=[Bxb[pb]])
            for d in range(8):
                for j in range(3):
                    psg, Bpg = pp.next()
                    for k in range(8):
                        P.add(PE, lambda e, k=k, j=j, d=d, psg=psg, pb=pb: e.matmul(
                            psg[:, :], wgt[:, k, j * 1024 + d * 128:j * 1024 + d * 128 + 128], hT[pb][:, k, :],
                            start=(k == 0), stop=(k == 7)), reads=[Bw, BhT[pb]], writes=[Bpg])
                    mbc = l * LV + V_MB + j * 8 + d
                    P.add(ACT, lambda e, psg=psg, j=j, mbc=mbc: e.activation(
                        out=gt[j][:], in_=psg[:, :], func=AF.Sigmoid, bias=C.vecs[:, mbc:mbc + 1]),
                        reads=[Bpg, C.Bconst], writes=[Bg[j]])
                    psy, Bpy = pp.next()
                    for k in range(nbr[j]):
                        P.add(PE, lambda e, k=k, j=j, d=d, psy=psy, pb=pb: e.matmul(
                            psy[:, :], wbr[j][:, k, d * 128:(d + 1) * 128], yin[j][pb][:, k, :],
                            start=(k == 0), stop=(k == nbr[j] - 1)), reads=[Bw, Byin[pb]], writes=[Bpy])
                    P.add(DVE, lambda e, psy=psy, j=j: e.tensor_tensor(out=tt[j][:], in0=gt[j][:], in1=psy[:, :], op=ALU.mult),
                          reads=[Bg[j], Bpy], writes=[Bt[j]])
                P.add(DVE, lambda e: e.tensor_tensor(out=tt[0][:], in0=tt[0][:], in1=tt[1][:], op=ALU.add),
                      reads=[Bt[0], Bt[1]], writes=[Bt[0]])
                P.add(DVE, lambda e, d=d: e.tensor_tensor(out=mT[:, d, :], in0=tt[0][:], in1=tt[2][:], op=ALU.add),
                      reads=[Bt[0], Bt[2]], writes=[BmT])
            for d in range(8):
                ps, Bp = pp.next()
                for k in range(8):
                    P.add(PE, lambda e, k=k, d=d, ps=ps: e.matmul(
                        ps[:, :], wot[:, k, d * 128:(d + 1) * 128], mT[:, k, :], start=(k == 0), stop=(k == 7)),
                        reads=[Bw, BmT], writes=[Bp])
                P.add(DVE, lambda e, d=d, ps=ps, pb=pb: e.tensor_tensor(
                    out=xb[pb][:, d, :], in0=ps[:, :], in1=xb[pb][:, d, :], op=ALU.add),
                    reads=[Bp, Bxb[pb]], writes=[Bxb[pb]])
            P.dma(SP, [(xv[:, :, t0:t0 + 512], xb[pb][:])], Bxb[pb], reads=[Bxb[pb]])
        P.barrier()


def stage_final(C):
    nc, P = C.nc, C.P
    xv = wview(C.xT)
    ov = wview(C.out)
    with ExitStack() as es:
        def sb(name, shape, dt):
            return es.enter_context(nc.sbuf_tensor(name + C.sfx, shape, dt))
        xb = [sb("z_xb%d" % i, [128, 8, 512], F32) for i in range(2)]
        yo = [sb("z_yo%d" % i, [128, 8, 512], F32) for i in range(2)]
        sq = sb("z_sq", [128, 8, 512], BF16)
        rstd = sb("z_rstd", [128, 512], F32)
        Bxb = [P.buf() for _ in range(2)]
        Byo = [P.buf() for _ in range(2)]
        Bsq, Brstd = P.buf(), P.buf()
        pp = PsumPool(C, es, 2, "z_ps")
        for n in range(T // 512):
            t0 = n * 512
            pb = n % 2
            P.dma(SP, [(xb[pb][:], xv[:, :, t0:t0 + 512])], Bxb[pb], writes=[Bxb[pb]])
            emit_norm(C, xb[pb][:], Bxb[pb], 512, V_FINAL, sq[:], Bsq, rstd[:], Brstd, yo[pb][:], Byo[pb], pp)
            P.dma(SP, [(ov[:, :, t0:t0 + 512], yo[pb][:])], Byo[pb], reads=[Byo[pb]])
        P.barrier()


INPUT_SPECS = [
    ("ffn1_w_in", (2, 1024, 5632)), ("ffn1_w_out", (2, 2816, 1024)), ("w_in", (2, 1024, 5376)),
    ("pool_w", (2, 4, 64, 64)), ("attn_lambda", (2, 4, 64)), ("attn_subln", (2, 128)),
    ("lru_w_a", (2, 2, 4, 64, 64)), ("lru_w_x", (2, 2, 4, 64, 64)),
    ("w_branch_pool", (2, 256, 1024)), ("w_branch_attn", (2, 512, 1024)), ("w_branch_lru", (2, 256, 1024)),
    ("w_out", (2, 1024, 1024)), ("ffn2_w_in", (2, 1024, 5632)), ("ffn2_w_out", (2, 2816, 1024)),
]


def build(stages=None, dbg=(), feed=()):
    nc = bass.Bass("TRN2", target_bir_lowering=False)
    C = Ctx()
    C.nc = nc
    C.P = Prog(nc)
    C.inp = {}
    for name, shape in INPUT_SPECS:
        C.inp[name] = nc.dram_tensor(name, list(shape), F32, kind="ExternalInput").ap()
    C.inp["vecs"] = nc.dram_tensor("vecs", [128, NV], F32, kind="ExternalInput").ap()
    C.inp["invc"] = nc.dram_tensor("invc", [2, 128, S], F32, kind="ExternalInput").ap()
    C.inp["kaug"] = nc.dram_tensor("kaug", [5, S], BF16, kind="ExternalInput").ap()
    C.inp["qaug"] = nc.dram_tensor("qaug", [4, 2, 5, S], BF16, kind="ExternalInput").ap()
    C.inp["cmat"] = nc.dram_tensor("cmat", [128, 7, 128], BF16, kind="ExternalInput").ap()
    dbg = set(dbg)

    def scratch(name, shape, dt):
        kind = "ExternalOutput" if name in dbg else ("ExternalInput" if name in feed else "Internal")
        return nc.dram_tensor(name, list(shape), dt, kind=kind).ap()

    C.xin = nc.dram_tensor("xin", [D, T], F32, kind="ExternalInput").ap()
    C.xT = scratch("xT", (D, T), F32)
    C.out = nc.dram_tensor("out", [D, T], F32, kind="ExternalOutput").ap()
    C.HT = scratch("HT", (D, T), BF16)
    C.PT = scratch("PT", (256, T), F32)
    C.QT = scratch("QT", (512, T), BF16)
    C.KT = scratch("KT", (512, T), BF16)
    C.V = scratch("V", (T, 512), BF16)
    C.LXT = scratch("LXT", (256, T), F32)
    C.LGT = scratch("LGT", (256, T), F32)
    C.YP = scratch("YP", (256, T), BF16)
    C.YA = scratch("YA", (512, T), BF16)
    C.YL = scratch("YL", (256, T), BF16)
    C.wb = {}
    for l in range(DEPTH):
        for name, shape in INPUT_SPECS:
            if name.startswith("w_") or name.startswith("ffn"):
                C.wb[(name, l)] = nc.dram_tensor("wb_%s_%d" % (name, l), list(shape[1:]), BF16).ap()

    with ExitStack() as es:
        P = C.P
        C.vecs = es.enter_context(nc.sbuf_tensor("c_vecs", [128, NV], F32))
        cm = es.enter_context(nc.sbuf_tensor("c_cmat", [128, 7, 128], BF16))
        C.epsn = es.enter_context(nc.sbuf_tensor("c_eps", [128, 4], F32))
        C.ones = cm[:, 0, :]
        C.ident = cm[:, 1, :]
        C.c0 = cm[:, 2, :]
        C.idS = cm[:, 3:7, :]
        C.Bconst = Buf("const")
        P.dma(SP, [(C.vecs[:], C.inp["vecs"]), (cm[:], C.inp["cmat"])], C.Bconst, writes=[C.Bconst])
        P.add(DVE, lambda e: e.memset(C.epsn[:, 0:1], NORM_EPS), writes=[C.Bconst])
        P.add(DVE, lambda e: e.memset(C.epsn[:, 1:2], 1.0), writes=[C.Bconst])
        for l in range(DEPTH):
            cf = 1.0 - (0.8 - 0.6 * math.exp(-0.3 * l))
            P.add(DVE, lambda e, l=l, cf=cf: e.memset(C.epsn[:, 2 + l:3 + l], SUBLN_EPS / (cf * cf)), writes=[C.Bconst])
        if C.xin is not None:
            Bcp = P.buf("xcopy")
            P.dma(SP, [(C.xT[r:r + 128, :], C.xin[r:r + 128, :]) for r in range(0, D, 128)], Bcp, writes=[Bcp])
        P.barrier()

        def on(tag):
            C.sfx = "_" + tag
            return stages is None or tag in stages
        setup_weights(C)
        for l in range(DEPTH):
            if on("F1_%d" % l):
                stage_ffn(C, l, 1)
            if on("I_%d" % l):
                stage_inproj(C, l)
            if on("P_%d" % l):
                stage_pool(C, l)
            if on("R_%d" % l):
                stage_lru(C, l)
            if on("A_%d" % l):
                stage_attn(C, l)
            if on("M_%d" % l):
                stage_merge(C, l)
            if on("F2_%d" % l):
                stage_ffn(C, l, 2)
        if on("Z"):
            stage_final(C)
        P.emit(es)
    return nc, C


def chunkvec(v):
    v = np.asarray(v, np.float32)
    return np.ascontiguousarray(v.reshape(-1, 128).T)


def host_tables(inputs):
    vecs = np.zeros((128, NV), np.float32)
    for l in range(DEPTH):
        b = l * LV
        vecs[:, b + V_FFN1:b + V_FFN1 + 8] = chunkvec(inputs["ffn1_norm"][l])
        vecs[:, b + V_MIX:b + V_MIX + 8] = chunkvec(inputs["mix_norm"][l])
        vecs[:, b + V_FFN2:b + V_FFN2 + 8] = chunkvec(inputs["ffn2_norm"][l])
        vecs[:, b + V_PSCALE:b + V_PSCALE + 2] = chunkvec(inputs["pool_scale"][l])
        for i in range(4):
            vecs[:, b + V_CONVW + 2 * i:b + V_CONVW + 2 * i + 2] = chunkvec(inputs["lru_conv_w"][l, i])
        vecs[:, b + V_CONVB:b + V_CONVB + 2] = chunkvec(inputs["lru_conv_b"][l])
        for dr in range(2):
            vecs[:, b + V_BA + 2 * dr:b + V_BA + 2 * dr + 2] = chunkvec(inputs["lru_b_a"][l, dr])
            vecs[:, b + V_BX + 2 * dr:b + V_BX + 2 * dr + 2] = chunkvec(inputs["lru_b_x"][l, dr])
            vecs[:, b + V_LAM + 2 * dr:b + V_LAM + 2 * dr + 2] = chunkvec(inputs["lru_lambda"][l, dr])
        for j in range(3):
            vecs[:, b + V_MB + 8 * j:b + V_MB + 8 * j + 8] = chunkvec(inputs["merge_bias"][l, j])
    vecs[:, V_FINAL:V_FINAL + 8] = chunkvec(inputs["final_norm"])
    t = np.arange(S)
    invc = np.zeros((2, 128, S), np.float32)
    for g, w in enumerate(POOL_WINDOWS):
        lo = np.clip(t - w // 2, 0, S)
        hi = np.clip(t + w - w // 2, 0, S)
        invc[g // 2, (g % 2) * 64:(g % 2) * 64 + 64, :] = (1.0 / (hi - lo).astype(np.float32))[None, :]
    kaug = np.zeros((5, S), np.float32)
    kaug[0:3] = 1.0
    kaug[3] = 128 * (t // 128)
    kaug[4] = t % 128
    qaug = np.zeros((4, 2, 5, S), np.float32)
    c = t % 512
    for h in range(4):
        for sg, sigma in enumerate((-1.0, 1.0)):
            sl = SLOPES[h]
            qaug[h, sg, 0] = sigma * sl * (512 * (t // 512))
            qaug[h, sg, 1] = sigma * sl * (c % 256)
            qaug[h, sg, 2] = sigma * sl * (256 * (c // 256))
            qaug[h, sg, 3] = -sigma * sl
            qaug[h, sg, 4] = -sigma * sl
    cmat = np.zeros((128, 7, 128), np.float32)
    cmat[:, 0, :] = 1.0
    cmat[:, 1, :] = np.eye(128)
    i = np.arange(128)
    cmat[:, 2, :] = -2.0 * np.maximum(i[:, None] - i[None, :], 0)
    for h in range(4):
        cmat[:, 3 + h, :] = SLOPES[h] * np.eye(128)
    bf = ml_dtypes.bfloat16
    return {"vecs": vecs, "invc": invc, "kaug": kaug.astype(bf), "qaug": qaug.astype(bf), "cmat": cmat.astype(bf)}


def make_in_maps(inputs, ncores=NCORES, xname="xin"):
    tabs = host_tables(inputs)
    shared = {name: np.ascontiguousarray(np.asarray(inputs[name], np.float32)) for name, _ in INPUT_SPECS}
    shared.update(tabs)
    x = np.asarray(inputs["x"], np.float32)
    maps = []
    for c in range(ncores):
        m = dict(shared)
        m[xname] = np.ascontiguousarray(x[2 * c:2 * c + 2].reshape(T, D).T)
        maps.append(m)
    return maps


_CACHE = {}


def kernel(**inputs):
    if "nc" not in _CACHE:
        _CACHE["nc"] = build()[0]
    nc = _CACHE["nc"]
    in_maps = make_in_maps(inputs)
    res = run_bass_kernel_spmd(nc, in_maps, core_ids=list(range(NCORES)))
    outs = []
    for c in range(NCORES):
        o = np.asarray(res.results[c]["out"], np.float32)
        outs.append(o.T.reshape(2, S, D))
    return np.ascontiguousarray(np.concatenate(outs, axis=0))
```
